# Optimizing a Trainium2 kernel written in Bass

```python
import math
import jax, jax.numpy as jnp
from jax import lax
import numpy as np

D_MODEL = 1024
BATCH = 16
SEQ = 2048
DEPTH = 1

HEAD_DIM = 64
HEADS_PER_GROUP = 4
ATTN_GROUPS = ((128, 1), (512, 4), (2048, 16))
N_ATTN_HEADS = HEADS_PER_GROUP * len(ATTN_GROUPS)
ATTN_OUT_W = HEADS_PER_GROUP * HEAD_DIM
ROPE_DIM = HEAD_DIM // 4
ROPE_THETA = 500000.0
BLOCK = 128
SSM_CH_PER_GROUP = 16
SSM_GROUPS = 32
SSM_W = SSM_CH_PER_GROUP * SSM_GROUPS
SSM_STATE = 64
D_FF = -(-8 * D_MODEL // (3 * 256)) * 256
QKV_W = 3 * N_ATTN_HEADS * HEAD_DIM
GATE_W = 2 * D_MODEL
IN_W = QKV_W + SSM_W + GATE_W
RMS_EPS = 1e-6
NEG_INF = -1e30

kernel_name = "hybrid_dilated_attn_s5_gated_block"


def rmsnorm(x, g):
    x32 = x.astype(jnp.float32)
    y = x32 * lax.rsqrt(jnp.mean(x32 * x32, axis=-1, keepdims=True) + RMS_EPS)
    return (y * g.astype(jnp.float32)).astype(x.dtype)


def partial_rope(t, pos):
    half = ROPE_DIM // 2
    inv = jnp.power(jnp.float32(ROPE_THETA), -jnp.arange(half, dtype=jnp.float32) * 2.0 / ROPE_DIM)
    ang = pos[:, None] * inv[None, :]
    cos = jnp.cos(ang)[None, :, None, :]
    sin = jnp.sin(ang)[None, :, None, :]
    tr = t[..., :ROPE_DIM].astype(jnp.float32)
    t1, t2 = tr[..., :half], tr[..., half:]
    rot = jnp.concatenate([t1 * cos - t2 * sin, t2 * cos + t1 * sin], axis=-1)
    return jnp.concatenate([rot.astype(t.dtype), t[..., ROPE_DIM:]], axis=-1)


def dilated_group_attention(q, k, v, window, dilation):
    B, S, H, E = q.shape
    span = window // dilation
    L = S // dilation
    nb = -(-L // BLOCK)
    Lp = nb * BLOCK

    def to_blocks(t):
        t = t.reshape(B, L, dilation, H, E).transpose(0, 2, 1, 3, 4)
        t = jnp.pad(t, ((0, 0), (0, 0), (0, Lp - L), (0, 0), (0, 0)))
        return t.reshape(B, dilation, nb, BLOCK, H, E)

    def with_prev(t):
        prev = jnp.pad(t[:, :, :-1], ((0, 0), (0, 0), (1, 0), (0, 0), (0, 0), (0, 0)))
        return jnp.concatenate([prev, t], axis=3)

    qb = to_blocks(q)
    kk = with_prev(to_blocks(k))
    vv = with_prev(to_blocks(v)).astype(jnp.float32)
    s = jnp.einsum('bdnqhe,bdnkhe->bdnhqk', qb, kk).astype(jnp.float32) * (HEAD_DIM ** -0.5)
    qi = jnp.arange(BLOCK)[:, None]
    ki = jnp.arange(2 * BLOCK)[None, :]
    dist = qi + BLOCK - ki
    band = (dist >= 0) & (dist <= span)
    blk = jnp.arange(nb)[:, None, None]
    valid = band[None] & ((blk > 0) | (ki >= BLOCK)[None])
    s = jnp.where(valid[None, None, :, None], s, NEG_INF)
    m = jnp.max(s, axis=-1, keepdims=True)
    p = jnp.exp(s - m)
    l = jnp.sum(p, axis=-1, keepdims=True)
    o = jnp.einsum('bdnhqk,bdnkhe->bdnhqe', p, vv) / l
    lse = (m + jnp.log(l))[..., 0]
    o = o.transpose(0, 1, 2, 4, 3, 5).reshape(B, dilation, Lp, H, E)[:, :, :L]
    o = o.transpose(0, 2, 1, 3, 4).reshape(B, S, H, E)
    lse = lse.transpose(0, 1, 2, 4, 3).reshape(B, dilation, Lp, H)[:, :, :L]
    lse = lse.transpose(0, 2, 1, 3).reshape(B, S, H)
    return o, lse


def s5_branch(u, a_re, a_im, log_dt, b_re, b_im, c_re, c_im, d_skip, w_glu):
    B, S, _ = u.shape
    u32 = u.astype(jnp.float32).reshape(B, S, SSM_GROUPS, SSM_CH_PER_GROUP)
    lr, li = a_re.astype(jnp.float32), a_im.astype(jnp.float32)
    dt = jnp.exp(log_dt.astype(jnp.float32))[:, None]
    mag = jnp.exp(lr * dt)
    ab_re, ab_im = mag * jnp.cos(li * dt), mag * jnp.sin(li * dt)
    den = lr * lr + li * li
    nr, ni = ab_re - 1.0, ab_im
    f_re = (nr * lr + ni * li) / den
    f_im = (ni * lr - nr * li) / den
    br, bi = b_re.astype(jnp.float32), b_im.astype(jnp.float32)
    bb_re = f_re[..., None] * br - f_im[..., None] * bi
    bb_im = f_re[..., None] * bi + f_im[..., None] * br
    bu_re = jnp.einsum('bsgc,gnc->bsgn', u32, bb_re)
    bu_im = jnp.einsum('bsgc,gnc->bsgn', u32, bb_im)
    a_r = jnp.broadcast_to(ab_re, bu_re.shape)
    a_i = jnp.broadcast_to(ab_im, bu_im.shape)

    def combine(e1, e2):
        a1r, a1i, b1r, b1i = e1
        a2r, a2i, b2r, b2i = e2
        return (a2r * a1r - a2i * a1i,
                a2r * a1i + a2i * a1r,
                a2r * b1r - a2i * b1i + b2r,
                a2r * b1i + a2i * b1r + b2i)

    _, _, xr, xi = lax.associative_scan(combine, (a_r, a_i, bu_re, bu_im), axis=1)
    y = (jnp.einsum('bsgn,gcn->bsgc', xr, c_re.astype(jnp.float32))
         - jnp.einsum('bsgn,gcn->bsgc', xi, c_im.astype(jnp.float32))
         + d_skip.astype(jnp.float32) * u32)
    y = jax.nn.gelu(y.reshape(B, S, SSM_W)).astype(u.dtype)
    z = y @ w_glu
    za, zb = z[..., :D_MODEL], z[..., D_MODEL:]
    return za * jax.nn.sigmoid(zb)


def setup_inputs(seed: int = 0) -> dict:
    key = jax.random.key(seed)
    ks = jax.random.split(key, 24)
    f32 = jnp.float32
    nrm = lambda k, shape, scale: jax.random.normal(k, shape, f32) * scale
    x = jax.random.normal(ks[0], (BATCH, SEQ, D_MODEL), f32)
    norm_mix_g = 1.0 + nrm(ks[1], (DEPTH, D_MODEL), 0.05)
    w_in = nrm(ks[2], (DEPTH, D_MODEL, IN_W), D_MODEL ** -0.5)
    n_idx = jnp.arange(SSM_STATE, dtype=f32)
    ssm_a_re = -0.5 * jnp.exp(nrm(ks[3], (DEPTH, SSM_GROUPS, SSM_STATE), 0.05))
    ssm_a_im = math.pi * n_idx + nrm(ks[4], (DEPTH, SSM_GROUPS, SSM_STATE), 0.01)
    ssm_log_dt = jax.random.uniform(ks[5], (DEPTH, SSM_GROUPS), f32, math.log(1e-3), math.log(1e-1))
    ssm_b_re = nrm(ks[6], (DEPTH, SSM_GROUPS, SSM_STATE, SSM_CH_PER_GROUP), (2.0 * SSM_CH_PER_GROUP) ** -0.5)
    ssm_b_im = nrm(ks[7], (DEPTH, SSM_GROUPS, SSM_STATE, SSM_CH_PER_GROUP), (2.0 * SSM_CH_PER_GROUP) ** -0.5)
    ssm_c_re = nrm(ks[8], (DEPTH, SSM_GROUPS, SSM_CH_PER_GROUP, SSM_STATE), SSM_STATE ** -0.5)
    ssm_c_im = nrm(ks[9], (DEPTH, SSM_GROUPS, SSM_CH_PER_GROUP, SSM_STATE), SSM_STATE ** -0.5)
    ssm_d = nrm(ks[10], (DEPTH, SSM_GROUPS, SSM_CH_PER_GROUP), 1.0)
    w_glu = nrm(ks[11], (DEPTH, SSM_W, 2 * D_MODEL), SSM_W ** -0.5)
    w_attn_out = nrm(ks[12], (DEPTH, ATTN_OUT_W, D_MODEL), ATTN_OUT_W ** -0.5)
    w_out = nrm(ks[13], (DEPTH, D_MODEL, D_MODEL), D_MODEL ** -0.5)
    norm_ffn_g = 1.0 + nrm(ks[14], (DEPTH, D_MODEL), 0.05)
    w_ffn_gate = nrm(ks[15], (DEPTH, D_MODEL, D_FF), D_MODEL ** -0.5)
    w_ffn_up = nrm(ks[16], (DEPTH, D_MODEL, D_FF), D_MODEL ** -0.5)
    w_ffn_down = nrm(ks[17], (DEPTH, D_FF, D_MODEL), D_FF ** -0.5)
    norm_final_g = 1.0 + nrm(ks[18], (D_MODEL,), 0.05)
    return {"x": x, "norm_mix_g": norm_mix_g, "w_in": w_in,
            "ssm_a_re": ssm_a_re, "ssm_a_im": ssm_a_im, "ssm_log_dt": ssm_log_dt,
            "ssm_b_re": ssm_b_re, "ssm_b_im": ssm_b_im, "ssm_c_re": ssm_c_re,
            "ssm_c_im": ssm_c_im, "ssm_d": ssm_d, "w_glu": w_glu,
            "w_attn_out": w_attn_out, "w_out": w_out, "norm_ffn_g": norm_ffn_g,
            "w_ffn_gate": w_ffn_gate, "w_ffn_up": w_ffn_up, "w_ffn_down": w_ffn_down,
            "norm_final_g": norm_final_g}


def reference(x, norm_mix_g, w_in, ssm_a_re, ssm_a_im, ssm_log_dt, ssm_b_re, ssm_b_im,
              ssm_c_re, ssm_c_im, ssm_d, w_glu, w_attn_out, w_out, norm_ffn_g,
              w_ffn_gate, w_ffn_up, w_ffn_down, norm_final_g):
    B, S, D = x.shape
    pos = jnp.arange(S, dtype=jnp.float32)
    for layer in range(DEPTH):
        h = rmsnorm(x, norm_mix_g[layer])
        proj = h @ w_in[layer]
        qkv = proj[..., :QKV_W].reshape(B, S, 3, N_ATTN_HEADS, HEAD_DIM)
        u = proj[..., QKV_W:QKV_W + SSM_W]
        gate = jax.nn.sigmoid(proj[..., QKV_W + SSM_W:].astype(jnp.float32)).reshape(B, S, 2, D)
        q = partial_rope(qkv[:, :, 0], pos)
        k = partial_rope(qkv[:, :, 1], pos)
        v = qkv[:, :, 2]
        outs, lses = [], []
        for gi, (window, dilation) in enumerate(ATTN_GROUPS):
            sl = slice(gi * HEADS_PER_GROUP, (gi + 1) * HEADS_PER_GROUP)
            o_g, lse_g = dilated_group_attention(q[:, :, sl], k[:, :, sl], v[:, :, sl], window, dilation)
            outs.append(o_g)
            lses.append(lse_g)
        outs = jnp.stack(outs, axis=0)
        alpha = jax.nn.softmax(jnp.stack(lses, axis=0), axis=0)
        attn = jnp.sum(alpha[..., None] * outs, axis=0).reshape(B, S, ATTN_OUT_W).astype(x.dtype)
        attn_d = attn @ w_attn_out[layer]
        ssm_out = s5_branch(u, ssm_a_re[layer], ssm_a_im[layer], ssm_log_dt[layer],
                            ssm_b_re[layer], ssm_b_im[layer], ssm_c_re[layer], ssm_c_im[layer],
                            ssm_d[layer], w_glu[layer])
        merged = (gate[:, :, 0] * attn_d.astype(jnp.float32)
                  + gate[:, :, 1] * ssm_out.astype(jnp.float32)).astype(x.dtype)
        x = x + merged @ w_out[layer]
        h2 = rmsnorm(x, norm_ffn_g[layer])
        ff = (jax.nn.silu(h2 @ w_ffn_gate[layer]) * (h2 @ w_ffn_up[layer])) @ w_ffn_down[layer]
        x = x + ff
    return rmsnorm(x, norm_final_g)
```

```python
import math
from contextlib import ExitStack

import numpy as np
import ml_dtypes

import concourse.bass as bass
import concourse.mybir as mybir
from concourse.bass_utils import run_bass_kernel_spmd

F32 = mybir.dt.float32
BF16 = mybir.dt.bfloat16
AF = mybir.ActivationFunctionType
ALU = mybir.AluOpType

NCORES = 8
S = 2048
D = 1024
NSEQ = 2
DFF = 2816
NFC = DFF // 128
GROUPS = ((128, 1), (512, 4), (2048, 16))
TWO_PI = 2.0 * math.pi
MAGIC = 12582912.0
NJ = 23


class Res:
    __slots__ = ("name", "w", "r", "sem", "semv")

    def __init__(self, name, sem=None):
        self.name = name
        self.w = None
        self.r = []
        self.sem = sem
        self.semv = 0


class Prog:
    ENG = ("pe", "act", "dve", "pool", "sp")

    def __init__(self, nc, stack):
        self.nc = nc
        self.stack = stack
        self.sem = {e: stack.enter_context(nc.semaphore("s_" + e)) for e in self.ENG}
        self.cnt = {e: 0 for e in self.ENG}
        self.pending = {e: False for e in self.ENG}
        self.waited = {e: {} for e in self.ENG}
        self.ops = {e: [] for e in self.ENG}
        self.slots = []

    def res(self, name, dma=False):
        sem = None
        if dma:
            sem = self.stack.enter_context(self.nc.semaphore("d_" + name))
        r = Res(name, sem)
        if dma:
            self.slots.append(r)
        return r

    def _wait(self, eng, tok):
        sem, val = tok
        key = id(sem)
        if self.waited[eng].get(key, 0) >= val:
            return
        self.waited[eng][key] = val
        self.ops[eng].append(("wait", sem, val))

    def _deps(self, eng, reads, writes):
        pes = self.sem["pe"]
        for r in reads:
            if r.w is not None and not (eng == "pe" and r.w[0] is pes):
                self._wait(eng, r.w)
        for w in writes:
            if w.w is not None and not (eng == "pe" and w.w[0] is pes):
                self._wait(eng, w.w)
            for t in w.r:
                if not (eng == "pe" and t[0] is pes):
                    self._wait(eng, t)

    def op(self, eng, fn, reads=(), writes=(), signal=True):
        self._deps(eng, reads, writes)
        tok = (self.sem[eng], self.cnt[eng] + 1)
        if signal:
            self.cnt[eng] += 1
            self.pending[eng] = False
        else:
            self.pending[eng] = True
        self.ops[eng].append(("op", fn, signal))
        for r in reads:
            r.r.append(tok)
        for w in writes:
            w.w = tok
            w.r = []

    def dma(self, eng, out, in_, slot, reads=(), writes=()):
        self._deps(eng, reads, writes)
        slot.semv += 16
        tok = (slot.sem, slot.semv)
        self.ops[eng].append(("dma", out, in_, slot.sem))
        for r in reads:
            r.r.append(tok)
        for w in writes:
            w.w = tok
            w.r = []
        return tok

    def barrier(self):
        toks = [(self.sem[e], self.cnt[e]) for e in self.ENG if self.cnt[e] > 0]
        toks += [(s.sem, s.semv) for s in self.slots if s.semv > 0]
        for e in self.ENG:
            assert not self.pending[e]
            for t in toks:
                if t[0] is not self.sem[e]:
                    self._wait(e, t)

    def emit(self):
        nc = self.nc
        for e in self.ENG:
            assert not self.pending[e], e
        if not any(self.ops[e] for e in self.ENG):
            return
        with nc.allow_non_contiguous_dma(reason="small strided setup loads"), nc.Block() as block:
            def run(engname, engobj):
                sem_self = self.sem[engname]
                for o in self.ops[engname]:
                    if o[0] == "wait":
                        engobj.wait_ge(o[1], o[2])
                    elif o[0] == "op":
                        ins = o[1](engobj)
                        if o[2]:
                            ins.then_inc(sem_self, 1)
                    else:
                        engobj.dma_start(out=o[1], in_=o[2]).then_inc(o[3], 16)

            @block.tensor
            def _(e):
                run("pe", e)

            @block.scalar
            def _(e):
                run("act", e)

            @block.vector
            def _(e):
                run("dve", e)

            @block.gpsimd
            def _(e):
                run("pool", e)

            @block.sync
            def _(e):
                run("sp", e)
        self.ops = {e: [] for e in self.ENG}


class T:
    def __init__(self, t, r):
        self.t = t
        self.r = r

    def __getitem__(self, k):
        return self.t[k]


class _Stop(Exception):
    pass


def build_program(debug=False, stop=None):
    nc = bass.Bass("TRN2", target_bir_lowering=False)

    def din(name, shape, dt=F32):
        return nc.dram_tensor(name, list(shape), dt, kind="ExternalInput").ap()

    x = din("x", [NSEQ * S, D])
    wqkv = din("wqkv", [3, D, 768])
    wu = din("wu", [D, 512])
    wgate = din("wgate", [D, 2048])
    wglu = din("wglu", [512, 2048])
    wao = din("wao", [256, D])
    wout = din("wout", [D, D])
    wfg = din("wfg", [D, DFF])
    wfu = din("wfu", [D, DFF])
    wfd = din("wfd", [DFF, D])
    gains = din("gains", [3, D])
    a_re = din("a_re", [32, 64])
    a_im = din("a_im", [32, 64])
    log_dt = din("log_dt", [1, 32])
    b_re = din("b_re", [32, 64, 16])
    b_im = din("b_im", [32, 64, 16])
    c_re = din("c_re", [512, 64])
    c_im = din("c_im", [512, 64])
    dsk = din("dsk", [32, 16])
    c_ident = din("c_ident", [128, 128])
    c_maskc = din("c_maskc", [128, 128], BF16)
    c_maskp = din("c_maskp", [128, 128], BF16)
    c_ropec = din("c_ropec", [128, 768])
    c_ropes = din("c_ropes", [128, 768])
    c_jv = din("c_jv", [128, NJ * 32])
    c_mrow = din("c_mrow", [128, 256])
    c_step = din("c_step", [128, 256])
    c_tmask = din("c_tmask", [128, 128])
    out = nc.dram_tensor("out", [NSEQ * S, D], F32, kind="ExternalOutput").ap()
    x1buf = nc.dram_tensor("x1buf", [NSEQ * S, D], F32, kind=("ExternalOutput" if debug else "Internal")).ap()
    ssm_mats = nc.dram_tensor("ssm_mats", [4, 128, 4096], BF16).ap()
    ssm_tab = nc.dram_tensor("ssm_tab", [3, 128, 8192], F32).ap()
    dbg_attn = dbg_y = None
    if debug:
        dbg_attn = nc.dram_tensor("dbg_attn", [128, 2 * S], BF16, kind="ExternalOutput").ap()
        dbg_y = nc.dram_tensor("dbg_y", [128, 4 * S], BF16, kind="ExternalOutput").ap()

    try:
      with ExitStack() as top:
        P = Prog(nc, top)

        def maybe_stop(tag):
            if stop == tag:
                raise _Stop()

        def cut(tag):
            if stop == tag:
                P.barrier()
                P.emit()
                raise _Stop()

        uid = [0]

        def sb(st, name, shape, dt, dma=False):
            uid[0] += 1
            name = f"{name}_{uid[0]}"
            t = st.enter_context(nc.sbuf_tensor(name, list(shape), dt))
            return T(t, P.res(name, dma=dma))

        def R(ts):
            return [t.r if isinstance(t, T) else t for t in ts]

        def MM(o, lhsT, rhs, start, stop, rd, wr, signal=True):
            P.op("pe", lambda e: e.matmul(o, lhsT=lhsT, rhs=rhs, start=start, stop=stop),
                 R(rd), R(wr), signal)

        def TR(o, in_, ident, rd, wr, signal=True):
            P.op("pe", lambda e: e.transpose(o, in_, ident), R(rd), R(wr), signal)

        def ACT(o, in_, func, rd, wr, scale=1.0, bias=0.0, accum_out=None):
            if accum_out is None:
                P.op("act", lambda e: e.activation(out=o, in_=in_, func=func, bias=bias, scale=scale),
                     R(rd), R(wr))
            else:
                P.op("act", lambda e: e.activation(out=o, in_=in_, func=func, bias=bias, scale=scale,
                                                   accum_out=accum_out), R(rd), R(wr))

        def CP(eng, o, in_, rd, wr):
            if eng == "act":
                ACT(o, in_, AF.Copy, rd, wr)
            else:
                P.op(eng, lambda e: e.tensor_copy(out=o, in_=in_), R(rd), R(wr))

        def TT(eng, o, in0, in1, op, rd, wr):
            P.op(eng, lambda e: e.tensor_tensor(out=o, in0=in0, in1=in1, op=op), R(rd), R(wr))

        def TS(eng, o, in0, s1, s2, op0, op1, rd, wr):
            if s2 is None:
                P.op(eng, lambda e: e.tensor_scalar(out=o, in0=in0, scalar1=s1, scalar2=None, op0=op0),
                     R(rd), R(wr))
            else:
                P.op(eng, lambda e: e.tensor_scalar(out=o, in0=in0, scalar1=s1, scalar2=s2, op0=op0, op1=op1),
                     R(rd), R(wr))

        def STT(eng, o, in0, sc, in1, op0, op1, rd, wr):
            P.op(eng, lambda e: e.scalar_tensor_tensor(out=o, in0=in0, scalar=sc, in1=in1, op0=op0, op1=op1),
                 R(rd), R(wr))

        def MSET(eng, o, val, wr):
            P.op(eng, lambda e: e.memset(o, val), [], R(wr))

        def RECIP(o, in_, rd, wr):
            P.op("dve", lambda e: e.reciprocal(out=o, in_=in_), R(rd), R(wr))

        def DMA(eng, o, in_, slot, rd=(), wr=()):
            return P.dma(eng, o, in_, slot.r if isinstance(slot, T) else slot, R(rd), R(wr))

        banks = []
        for i in range(8):
            t = top.enter_context(nc.psum_tensor(f"bank{i}", [128, 512], F32))
            banks.append(T(t, P.res(f"bank{i}")))

        def bfv(bank):
            return bank.t[:].bitcast(BF16)

        ident32 = sb(top, "ident32", [128, 128], F32, dma=True)
        identb = sb(top, "identb", [128, 128], BF16)
        maskc = sb(top, "maskc", [128, 128], BF16, dma=True)
        maskp = sb(top, "maskp", [128, 128], BF16, dma=True)
        ropec = sb(top, "ropec", [128, 768], F32, dma=True)
        ropes = sb(top, "ropes", [128, 768], F32, dma=True)
        gain = sb(top, "gain", [128, 3 * D], F32, dma=True)
        DMA("sp", ident32[:], c_ident, ident32, wr=[ident32])
        DMA("sp", maskc[:], c_maskc, maskc, wr=[maskc])
        DMA("sp", maskp[:], c_maskp, maskp, wr=[maskp])
        DMA("sp", ropec[:], c_ropec, ropec, wr=[ropec])
        DMA("sp", ropes[:], c_ropes, ropes, wr=[ropes])
        for i in range(3):
            DMA("sp", gain[:, i * D:(i + 1) * D], gains[i:i + 1, :].partition_broadcast(128), gain, wr=[gain])
        CP("dve", identb[:], ident32[:], [ident32], [identb])
        r_mats = P.res("ssm_mats", dma=True)
        r_tab = P.res("ssm_tab", dma=True)
        r_x1 = P.res("x1buf", dma=True)
        dbg_slot = P.res("dbg", dma=True)
        out_toks = []

        def rmsnorm_tile(xt, gidx, h_out, wk, rd_extra=()):
            junk, ss, ss2, rstd = wk
            MSET("dve", ss[:, 0:1], 0.0, [ss])
            ACT(junk[:], xt[:], AF.Square, [xt], [junk, ss], accum_out=ss[:, 0:1])
            TS("dve", ss2[:, 0:1], ss[:, 0:1], 1.0 / D, 1e-6, ALU.mult, ALU.add, [ss], [ss2])
            ACT(ss2[:, 0:1], ss2[:, 0:1], AF.Sqrt, [ss2], [ss2])
            RECIP(rstd[:, 0:1], ss2[:, 0:1], [ss2], [rstd])
            STT("dve", h_out[:], xt[:], rstd[:, 0:1], gain[:, gidx * D:(gidx + 1) * D], ALU.mult, ALU.mult,
                [xt, rstd, gain], [h_out])

        with ExitStack() as st:
            A1 = sb(st, "A1", [32, 64], F32, dma=True)
            A2 = sb(st, "A2", [32, 64], F32, dma=True)
            DMA("sp", A1[:], a_re, A1, wr=[A1])
            DMA("sp", A2[:], a_im, A2, wr=[A2])
            lr = sb(st, "lr", [128, 32], F32)
            li = sb(st, "li", [128, 32], F32)
            dt_ = sb(st, "dt_", [128, 32], F32, dma=True)
            DMA("sp", dt_[:], log_dt.partition_broadcast(128), dt_, wr=[dt_])
            for (src, dst, bk) in ((A1, lr, banks[0]), (A2, li, banks[1])):
                TR(bk[0:64, 0:32], src[:], ident32[0:32, 0:32], [src, ident32], [bk])
                CP("act", dst[0:64, :], bk[0:64, 0:32], [bk], [dst])
                CP("act", dst[64:128, :], bk[0:64, 0:32], [bk], [dst])
            cut("s1")
            ACT(dt_[:], dt_[:], AF.Exp, [dt_], [dt_])
            lrdt = sb(st, "lrdt", [128, 32], F32)
            th = sb(st, "th", [128, 32], F32)
            TT("dve", lrdt[:], lr[:], dt_[:], ALU.mult, [lr, dt_], [lrdt])
            TT("dve", th[:], li[:], dt_[:], ALU.mult, [li, dt_], [th])
            jv = sb(st, "jv", [128, NJ * 32], F32, dma=True)
            DMA("sp", jv[:], c_jv, jv, wr=[jv])
            jv3 = jv[:].rearrange("p (j g) -> p j g", j=NJ)
            PR = sb(st, "PR", [128, NJ * 32], F32)
            PI = sb(st, "PI", [128, NJ * 32], F32)
            MAG = sb(st, "MAG", [128, NJ * 32], F32)
            ANG = sb(st, "ANG", [128, NJ * 32], F32)
            RT = sb(st, "RT", [128, NJ * 32], F32)

            def v3(t):
                return t[:].rearrange("p (j g) -> p j g", j=NJ)

            def bc_g(t):
                return t[:].unsqueeze(1).to_broadcast([128, NJ, 32])

            def range_reduce(dst, src, tmp, shift, n):
                TS("dve", tmp[:, 0:n], src[:, 0:n], shift, 1.0 / TWO_PI, ALU.add, ALU.mult, [src], [tmp])
                TS("dve", tmp[:, 0:n], tmp[:, 0:n], MAGIC, -MAGIC, ALU.add, ALU.add, [tmp], [tmp])
                TS("dve", tmp[:, 0:n], tmp[:, 0:n], -TWO_PI, shift, ALU.mult, ALU.add, [tmp], [tmp])
                TT("dve", dst[:, 0:n], tmp[:, 0:n], src[:, 0:n], ALU.add, [tmp, src], [dst])
                TS("dve", dst[:, 0:n], dst[:, 0:n], 3.14159, -3.14159, ALU.min, ALU.max, [dst], [dst])

            TT("dve", v3(MAG), jv3, bc_g(lrdt), ALU.mult, [jv, lrdt], [MAG])
            ACT(MAG[:], MAG[:], AF.Exp, [MAG], [MAG])
            TT("dve", v3(ANG), jv3, bc_g(th), ALU.mult, [jv, th], [ANG])
            n_all = NJ * 32
            range_reduce(PI, ANG, RT, 0.0, n_all)
            ACT(PI[:], PI[:], AF.Sin, [PI], [PI])
            range_reduce(PR, ANG, RT, math.pi / 2, n_all)
            ACT(PR[:], PR[:], AF.Sin, [PR], [PR])
            TT("dve", PR[:], PR[:], MAG[:], ALU.mult, [PR, MAG], [PR])
            TT("dve", PI[:], PI[:], MAG[:], ALU.mult, [PI, MAG], [PI])

            def pj(t, e):
                return t[:, (e + 7) * 32:(e + 8) * 32]

            cut("s2")
            f_re = sb(st, "f_re", [128, 32], F32)
            f_im = sb(st, "f_im", [128, 32], F32)
            w1 = sb(st, "w1", [128, 32], F32)
            w2 = sb(st, "w2", [128, 32], F32)
            w3 = sb(st, "w3", [128, 32], F32)
            nr = sb(st, "nr", [128, 32], F32)
            TS("dve", nr[:], pj(PR, 1), -1.0, None, ALU.add, None, [PR], [nr])
            TT("dve", w1[:], lr[:], lr[:], ALU.mult, [lr], [w1])
            TT("dve", w2[:], li[:], li[:], ALU.mult, [li], [w2])
            TT("dve", w1[:], w1[:], w2[:], ALU.add, [w1, w2], [w1])
            RECIP(w3[:], w1[:], [w1], [w3])
            TT("dve", w1[:], nr[:], lr[:], ALU.mult, [nr, lr], [w1])
            TT("dve", w2[:], pj(PI, 1), li[:], ALU.mult, [PI, li], [w2])
            TT("dve", w1[:], w1[:], w2[:], ALU.add, [w1, w2], [w1])
            TT("dve", f_re[:], w1[:], w3[:], ALU.mult, [w1, w3], [f_re])
            TT("dve", w1[:], pj(PI, 1), lr[:], ALU.mult, [PI, lr], [w1])
            TT("dve", w2[:], nr[:], li[:], ALU.mult, [nr, li], [w2])
            TT("dve", w1[:], w1[:], w2[:], ALU.subtract, [w1, w2], [w1])
            TT("dve", f_im[:], w1[:], w3[:], ALU.mult, [w1, w3], [f_im])

            cut("s3")
            Br = sb(st, "Br", [128, 512], F32, dma=True)
            Bi = sb(st, "Bi", [128, 512], F32, dma=True)
            for (src, dst) in ((b_re, Br), (b_im, Bi)):
                for half in range(2):
                    for gq in range(4):
                        DMA("sp", dst[half * 64:(half + 1) * 64, gq * 128:(gq + 1) * 128].rearrange("n (g c) -> n g c", g=8),
                            src[gq * 8:(gq + 1) * 8].rearrange("g n c -> n g c"), dst, wr=[dst])

            def g16(t):
                return t[:].rearrange("p (g c) -> p g c", g=32)

            def bc16(ap):
                return ap.unsqueeze(2).to_broadcast([128, 32, 16])

            BX = sb(st, "BX", [128, 512], F32)
            BY = sb(st, "BY", [128, 512], F32)
            t5 = sb(st, "t5", [128, 512], F32)
            t6 = sb(st, "t6", [128, 512], F32)
            Bbr = sb(st, "Bbr", [128, 512], F32)
            Bbi = sb(st, "Bbi", [128, 512], F32)
            TT("dve", g16(t5), g16(Br), bc16(f_re[:]), ALU.mult, [Br, f_re], [t5])
            TT("dve", g16(t6), g16(Bi), bc16(f_im[:]), ALU.mult, [Bi, f_im], [t6])
            TT("dve", Bbr[:], t5[:], t6[:], ALU.subtract, [t5, t6], [Bbr])
            TT("dve", g16(t5), g16(Bi), bc16(f_re[:]), ALU.mult, [Bi, f_re], [t5])
            TT("dve", g16(t6), g16(Br), bc16(f_im[:]), ALU.mult, [Br, f_im], [t6])
            TT("dve", Bbi[:], t5[:], t6[:], ALU.add, [t5, t6], [Bbi])
            CP("dve", BX[0:64, :], Bbr[0:64, :], [Bbr], [BX])
            CP("dve", BX[64:128, :], Bbi[64:128, :], [Bbi], [BX])
            TS("dve", BY[0:64, :], Bbi[0:64, :], -1.0, None, ALU.mult, None, [Bbi], [BY])
            CP("dve", BY[64:128, :], Bbr[64:128, :], [Bbr], [BY])

            cut("s4")
            CX = sb(st, "CX", [128, 512], F32)
            CY = sb(st, "CY", [128, 512], F32)
            cst = [sb(st, f"cst{i}", [128, 64], F32, dma=True) for i in range(2)]
            k = 0
            for (src, which) in ((c_re, 0), (c_im, 1)):
                for q4 in range(4):
                    stg = cst[k % 2]
                    bk = banks[2 + (k % 2)]
                    k += 1
                    DMA("sp", stg[:], src[q4 * 128:(q4 + 1) * 128, :], stg, wr=[stg])
                    TR(bk[0:64, 0:128], stg[:], ident32[:], [stg, ident32], [bk])
                    cs = slice(q4 * 128, (q4 + 1) * 128)
                    if which == 0:
                        CP("act", CX[0:64, cs], bk[0:64, 0:128], [bk], [CX])
                        ACT(CY[64:128, cs], bk[0:64, 0:128], AF.Copy, [bk], [CY], scale=-1.0)
                    else:
                        ACT(CX[64:128, cs], bk[0:64, 0:128], AF.Copy, [bk], [CX], scale=-1.0)
                        ACT(CY[0:64, cs], bk[0:64, 0:128], AF.Copy, [bk], [CY], scale=-1.0)

            cut("s5")
            stm = ExitStack()
            Q = sb(stm, "Q", [128, 4096], F32)
            t5e = sb(stm, "t5e", [128, 4096], F32)
            EEx = sb(stm, "EEx", [128, 8192], F32)
            Q4 = Q[:].rearrange("p (g s c) -> p g s c", g=32, s=8)
            E4 = EEx[:].rearrange("p (g e c) -> p g e c", g=32, e=16)
            for s_ in range(8):
                TT("dve", g16(t5), g16(BX), bc16(pj(PR, -s_)), ALU.mult, [BX, PR], [t5])
                TT("dve", g16(t6), g16(BY), bc16(pj(PI, -s_)), ALU.mult, [BY, PI], [t6])
                TT("dve", Q4[:, :, s_, :], g16(t5), g16(t6), ALU.add, [t5, t6], [Q])
            for e_ in range(16):
                TT("dve", g16(t5), g16(CX), bc16(pj(PR, e_)), ALU.mult, [CX, PR], [t5])
                TT("dve", g16(t6), g16(CY), bc16(pj(PI, e_)), ALU.mult, [CY, PI], [t6])
                TT("dve", E4[:, :, e_, :], g16(t5), g16(t6), ALU.add, [t5, t6], [EEx])
            Qhi = sb(stm, "Qhi", [128, 4096], BF16)
            Qlo = sb(stm, "Qlo", [128, 4096], BF16)
            Ehi = sb(stm, "Ehi", [128, 4096], BF16)
            Elo = sb(stm, "Elo", [128, 4096], BF16)
            E0v = E4[:, :, 0:8, :]
            Ehv = Ehi[:].rearrange("p (g e c) -> p g e c", g=32, e=8)
            Elv = Elo[:].rearrange("p (g e c) -> p g e c", g=32, e=8)
            CP("act", Qhi[:], Q[:], [Q], [Qhi])
            TT("dve", t5e[:], Q[:], Qhi[:], ALU.subtract, [Q, Qhi], [t5e])
            CP("act", Qlo[:], t5e[:], [t5e], [Qlo])
            CP("act", Ehv, E0v, [EEx], [Ehi])
            TT("dve", t5e[:].rearrange("p (g e c) -> p g e c", g=32, e=8), E0v, Ehv, ALU.subtract, [EEx, Ehi], [t5e])
            CP("act", Elo[:], t5e[:], [t5e], [Elo])

            cut("s6")
            dnat = sb(stm, "dnat", [32, 16], F32, dma=True)
            DMA("sp", dnat[:], dsk, dnat, wr=[dnat])
            drep = sb(stm, "drep", [32, 128], F32)
            CP("dve", drep[:].rearrange("g (s c) -> g s c", s=8), dnat[:].unsqueeze(1).to_broadcast([32, 8, 16]), [dnat], [drep])
            Dcol = sb(stm, "Dcol", [128, 32], F32)
            TR(banks[3][:, 0:32], drep[:], ident32[0:32, 0:32], [drep, ident32], [banks[3]])
            CP("act", Dcol[:], banks[3][:, 0:32], [banks[3]], [Dcol])
            tmask = sb(stm, "tmask", [128, 128], F32, dma=True)
            DMA("sp", tmask[:], c_tmask, tmask, wr=[tmask])

            cut("s6b")
            matsb = sb(stm, "matsb", [128, 4 * 4096], BF16)
            M4 = matsb[:].rearrange("p (k g c) -> p k g c", k=4, g=32)
            mres = [T(matsb.t, P.res(f"matsb_k{i}")) for i in range(4)]
            tmpm_l = [sb(stm, f"tmpm{i}", [128, 128], F32) for i in range(2)]
            tmpm0_l = [sb(stm, f"tmpm0{i}", [128, 128], F32) for i in range(2)]
            for g in range(32):
                gs_ = slice(g * 128, (g + 1) * 128)
                bk = banks[4 + (g % 2)]
                tmpm, tmpm0 = tmpm_l[g % 2], tmpm0_l[g % 2]
                bkv = bfv(bk)
                TR(bkv[:, 0:128], Qhi[:, gs_], identb[:], [Qhi, identb], [bk], signal=False)
                MM(bk[:, 256:384], Qhi[:, gs_], Ehi[:, gs_], True, False, [Qhi, Ehi], [bk], signal=False)
                MM(bk[:, 256:384], Qlo[:, gs_], Ehi[:, gs_], False, False, [Qlo, Ehi], [bk], signal=False)
                MM(bk[:, 256:384], Qhi[:, gs_], Elo[:, gs_], False, True, [Qhi, Elo], [bk])
                CP("act", M4[:, 0, g, :], bkv[:, 0:128], [bk], [mres[0]])
                CP("act", M4[:, 1, g, 0:64], bkv[:, 64:128], [bk], [mres[1]])
                CP("act", M4[:, 1, g, 64:128], bkv[:, 0:64], [bk], [mres[1]])
                if g == 0:
                    cut("s6c")
                CP("act", tmpm0[:], bk[:, 256:384], [bk], [tmpm0])
                TT("dve", tmpm[:], tmpm0[:], tmask[:], ALU.mult, [tmpm0, tmask], [tmpm])
                if g == 0:
                    cut("s6d")
                STT("dve", M4[:, 2, g, :], ident32[:], Dcol[:, g:g + 1], tmpm[:], ALU.mult, ALU.add,
                    [ident32, Dcol, tmpm], [mres[2]])
            cut("s7")
            CP("act", M4[:, 3, :, :].rearrange("p g (e c) -> p g e c", e=8), E4[:, :, 8:16, :], [EEx], [mres[3]])
            for k in range(4):
                DMA("sp", ssm_mats[k], matsb[:, k * 4096:(k + 1) * 4096], r_mats, rd=[mres[k]], wr=[r_mats])

            P.barrier()
            P.emit()
            stm.close()
            cut("s8")
            phi = sb(st, "phi", [128, 32], F32)
            r8 = sb(st, "r8", [128, 32], F32)
            TS("dve", phi[:], th[:], 8.0, None, ALU.mult, None, [th], [phi])
            range_reduce(phi, phi, w1, 0.0, 32)
            TS("dve", r8[:], lrdt[:], 8.0, None, ALU.mult, None, [lrdt], [r8])
            ACT(r8[:], r8[:], AF.Exp, [r8], [r8])
            mrow = sb(st, "mrow", [128, 256], F32, dma=True)
            step = sb(st, "step", [128, 256], F32, dma=True)
            DMA("sp", mrow[:], c_mrow, mrow, wr=[mrow])
            DMA("sp", step[:], c_step, step, wr=[step])
            angb = sb(st, "angb", [128, 2048], F32)
            halfpi = sb(st, "halfpi", [128, 1], F32)
            MSET("dve", halfpi[:], math.pi / 2, [halfpi])
            tb1 = sb(st, "tb1", [128, 2048], F32)
            tbo = [sb(st, f"tbo{i}", [128, 2048], F32) for i in range(3)]
            for gb in range(4):
                gs = slice(gb * 8, (gb + 1) * 8)
                phb = phi[:, gs].unsqueeze(2).to_broadcast([128, 8, 256])
                r8b = r8[:, gs].unsqueeze(2).to_broadcast([128, 8, 256])
                mb = mrow[:].unsqueeze(1).to_broadcast([128, 8, 256])
                stb = step[:].unsqueeze(1).to_broadcast([128, 8, 256])

                def v8(t):
                    return t[:].rearrange("p (g m) -> p g m", g=8)
                TT("dve", v8(angb), phb, mb, ALU.mult, [phi, mrow], [angb])

                def rr3(dst, shift):
                    TS("dve", tb1[:], angb[:], shift, 1.0 / TWO_PI, ALU.add, ALU.mult, [angb], [tb1])
                    TS("dve", tb1[:], tb1[:], MAGIC, -MAGIC, ALU.add, ALU.add, [tb1], [tb1])
                    STT("dve", dst[:], tb1[:], -TWO_PI, angb[:], ALU.mult, ALU.add, [tb1, angb], [dst])
                rr3(tbo[0], math.pi / 2)
                ACT(tbo[0][:], tbo[0][:], AF.Sin, [tbo[0]], [tbo[0]], bias=halfpi[:, 0:1])
                rr3(tbo[1], 0.0)
                ACT(tbo[1][0:64, :], tbo[1][0:64, :], AF.Sin, [tbo[1]], [tbo[1]])
                ACT(tbo[1][64:128, :], tbo[1][64:128, :], AF.Sin, [tbo[1]], [tbo[1]], scale=-1.0)
                TT("dve", v8(tbo[2]), r8b, stb, ALU.mult, [r8, step], [tbo[2]])
                for k in range(3):
                    DMA("sp", ssm_tab[k, :, gb * 2048:(gb + 1) * 2048], tbo[k][:], r_tab, rd=[tbo[k]], wr=[r_tab])
            P.barrier()
            P.emit()
            maybe_stop("setup")

        with ExitStack() as mix:
            h_fm = sb(mix, "h_fm", [128, 8 * S], BF16)
            attn_fm = sb(mix, "attn_fm", [128, 2 * S], BF16)
            yact_fm = sb(mix, "yact_fm", [128, 4 * S], BF16)
            H3 = h_fm[:].rearrange("p (k t) -> p k t", k=8)
            AT3 = attn_fm[:].rearrange("p (j t) -> p j t", j=2)
            YA3 = yact_fm[:].rearrange("p (k t) -> p k t", k=4)

            for q in range(NSEQ):
                row0 = q * S
                sseq = mix.enter_context(ExitStack())
                wus = sb(sseq, "wus", [128, 8 * 512], BF16, dma=True)
                WU3 = wus[:].rearrange("p (k c) -> p k c", k=8)
                mats = sb(sseq, "mats", [128, 4 * 4096], BF16, dma=True)
                MT4 = mats[:].rearrange("p (k g c) -> p k g c", k=4, g=32)
                sab = mix.enter_context(ExitStack())
                wg_sb = [sb(sab, f"wg{i}", [128, 8 * 768], BF16, dma=True) for i in range(3)]
                W3s = []
                for gi in range(3):
                    wg = wg_sb[gi]
                    W3 = wg[:].rearrange("p (k c) -> p k c", k=8)
                    DMA("pool", W3, wqkv[gi].rearrange("(k p) c -> p k c", p=128), wg, wr=[wg])
                    W3s.append(W3)
                DMA("pool", WU3, wu.rearrange("(k p) c -> p k c", p=128), wus, wr=[wus])
                for k in range(4):
                    DMA("sp", mats[:, k * 4096:(k + 1) * 4096], ssm_mats[k], mats, rd=[r_mats], wr=[mats])
                STAGE = "A"
                with ExitStack() as st:
                    xst = [sb(st, f"xst{i}", [128, D], F32, dma=True) for i in range(3)]
                    htm = [sb(st, f"htm{i}", [128, D], BF16) for i in range(2)]
                    junk_ = sb(st, "junk", [128, D], BF16)
                    wks = [(junk_, sb(st, f"ss{i}", [128, 1], F32), sb(st, f"ss2{i}", [128, 1], F32),
                            sb(st, f"rstd{i}", [128, 1], F32)) for i in range(2)]

                    def a_load(tt):
                        xs = xst[tt % 3]
                        DMA("sp", xs[:], x[row0 + tt * 128: row0 + (tt + 1) * 128, :], xs, wr=[xs])

                    def a_front(tt):
                        rmsnorm_tile(xst[tt % 3], 0, htm[tt % 2], wks[tt % 2])

                    def a_back(tt):
                        ht = htm[tt % 2]
                        bk = banks[6 + (tt % 2)]
                        bv = bfv(bk)
                        for kc in range(8):
                            TR(bv[:, kc * 128:(kc + 1) * 128], ht[:, kc * 128:(kc + 1) * 128], identb[:],
                               [ht, identb], [bk], signal=(kc == 7))
                        CP("act", H3[:, :, tt * 128:(tt + 1) * 128], bv.rearrange("p (k t) -> p k t", k=8),
                           [bk], [h_fm])
                    a_load(0)
                    a_load(1)
                    a_front(0)
                    for tt in range(16):
                        if tt + 2 < 16:
                            a_load(tt + 2)
                        if tt + 1 < 16:
                            a_front(tt + 1)
                        a_back(tt)
                    P.barrier()
                    P.emit()
                    maybe_stop(f"{STAGE}{q}")

                STAGE = "B"
                with ExitStack() as st:
                    qk32s = [sb(st, f"qk32r_{i}", [128, 128], F32) for i in range(3)]
                    qktms = [sb(st, f"qktm_{i}", [128, 512], BF16) for i in range(3)]
                    qkT = [sb(st, f"qkT{i}", [128, 512], BF16) for i in range(3)]
                    vaug = [sb(st, f"vaug{i}", [128, 512], BF16) for i in range(6)]
                    pps = [sb(st, f"pp{i}", [128, 1024], BF16) for i in range(2)]
                    mk = sb(st, "mk", [128, 256], BF16)
                    acc = sb(st, "acc", [128, 4 * S], F32)
                    onesE = sb(st, "onesE", [128, 128], BF16)
                    onesO = sb(st, "onesO", [128, 128], BF16)
                    rps = [[sb(st, f"rp{i}_{k}", [128, 128], F32) for i in range(2)] for k in range(3)]
                    AC4 = acc[:].rearrange("p (k t) -> p k t", k=4)
                    for v_ in vaug:
                        MSET("pool", v_[:], 0.0, [v_])
                    MSET("pool", onesE[:], 0.0, [onesE])
                    MSET("pool", onesO[:], 0.0, [onesO])
                    MSET("pool", onesE[:, 0:64], 1.0, [onesE])
                    MSET("pool", onesO[:, 64:128], 1.0, [onesO])
                    CP("pool", mk[:, 0:128], maskc[:], [maskc], [mk])
                    CP("pool", mk[:, 128:256], maskp[:], [maskp], [mk])
                    blocks = []
                    for gi, (window, dil) in enumerate(GROUPS):
                        nb = (S // dil) // 128
                        for r_ in range(dil):
                            for n_ in range(nb):
                                blocks.append((gi, dil, r_, n_, r_ * nb + n_, len(blocks)))
                    pbanks = [(banks[0], banks[1]), (banks[6], banks[7])]
                    bSc = [banks[3], banks[4]]
                    PP5s = [p_[:].rearrange("p (c j two q) -> p c j two q", c=2, j=2, two=2) for p_ in pps]
                    PP4s = [p_[:].rearrange("p (c h q) -> p c h q", c=2, h=4) for p_ in pps]

                    def tokv(B, ap3):
                        gi, dil, r_, n_, blk, ix = B
                        return ap3.rearrange("p k (n i d) -> p k d n i", d=dil, i=128)[:, :, r_, n_, :]

                    def proj(B):
                        gi, dil, r_, n_, blk, ix = B
                        wg, W3 = wg_sb[gi], W3s[gi]
                        hblk = tokv(B, H3)
                        bA, bB = pbanks[ix % 2]
                        for kc in range(8):
                            MM(bA[:, 0:384], hblk[:, kc, :], W3[:, kc, 0:384], kc == 0, kc == 7,
                               [h_fm, wg], [bA], signal=False)
                        for kc in range(8):
                            MM(bB[:, 0:384], hblk[:, kc, :], W3[:, kc, 384:768], kc == 0, kc == 7,
                               [h_fm, wg], [bB], signal=(kc == 7))

                    def evac_rope(B):
                        gi, dil, r_, n_, blk, ix = B
                        bA, bB = pbanks[ix % 2]
                        q32, qktm, rp = qk32s[ix % 3], qktms[ix % 3], rps[ix % 3]
                        CP("act", qktm[:, 0:384], bA[:, 0:384], [bA], [qktm])
                        CP("act", qktm[:, 384:512], bB[:, 0:128], [bB], [qktm])
                        va = vaug[ix % 6]
                        VA4 = va[:].rearrange("p (j two c) -> p j two c", j=2, two=2)
                        vsrc = bB[:, 128:384].rearrange("p (j two e) -> p j two e", j=2, two=2)
                        CP("act", VA4[:, :, 0, 0:64], vsrc[:, :, 0, :], [bB], [va])
                        CP("act", VA4[:, :, 1, 64:128], vsrc[:, :, 1, :], [bB], [va])
                        q8 = q32[:].rearrange("p (h e) -> p h e", h=8)
                        CP("act", q8[:, 0:6, :], bA[:, 0:384].rearrange("p (h e) -> p h e", h=6)[:, :, 0:16], [bA], [q32])
                        CP("act", q8[:, 6:8, :], bB[:, 0:128].rearrange("p (h e) -> p h e", h=2)[:, :, 0:16], [bB], [q32])
                        o8 = qktm[:].rearrange("p (h e) -> p h e", h=8)
                        tix = (gi * 16 + blk) * 16
                        cc = ropec[:, tix:tix + 16].unsqueeze(1).to_broadcast([128, 8, 16])
                        ss_lo = ropes[:, tix:tix + 8].unsqueeze(1).to_broadcast([128, 8, 8])
                        ss_hi = ropes[:, tix + 8:tix + 16].unsqueeze(1).to_broadcast([128, 8, 8])
                        ma = rp[0][:].rearrange("p (h e) -> p h e", h=8)
                        mb = rp[1][:].rearrange("p (h e) -> p h e", h=8)
                        TT("dve", ma, q8, cc, ALU.mult, [q32, ropec], [rp[0]])
                        TT("dve", mb[:, :, 0:8], q8[:, :, 8:16], ss_lo, ALU.mult, [q32, ropes], [rp[1]])
                        TT("dve", mb[:, :, 8:16], q8[:, :, 0:8], ss_hi, ALU.mult, [q32, ropes], [rp[1]])
                        TT("dve", o8[:, :, 0:16], ma, mb, ALU.add, [rp[0], rp[1]], [qktm])

                    def transp(B):
                        gi, dil, r_, n_, blk, ix = B
                        qktm = qktms[ix % 3]
                        bT = banks[2]
                        bTv = bfv(bT)
                        for j in range(4):
                            TR(bTv[:, j * 128:(j + 1) * 128], qktm[:, j * 128:(j + 1) * 128], identb[:],
                               [qktm, identb], [bT], signal=(j == 3))
                        CP("dve", qkT[ix % 3][:], bTv[:, 0:512], [bT], [qkT[ix % 3]])

                    def scores(B):
                        gi, dil, r_, n_, blk, ix = B
                        kc_, kp_ = qkT[ix % 3], qkT[(ix - 1) % 3]
                        srcs = [(kc_, 0)] + ([(kp_, 256)] if n_ > 0 else [])
                        nmm = 4 * len(srcs)
                        cnt_ = 0
                        for (kt, off) in srcs:
                            for h in range(4):
                                pb = (h % 2) * 64
                                jj = h // 2
                                qs = kc_[pb:pb + 64, jj * 128:(jj + 1) * 128]
                                kk_ = kt[pb:pb + 64, (2 + jj) * 128:(3 + jj) * 128]
                                cnt_ += 1
                                MM(bSc[h % 2][:, off + jj * 128:off + (jj + 1) * 128], kk_, qs, True, True,
                                   [kc_, kt], [bSc[h % 2]], signal=(cnt_ >= nmm - 1))

                    def softmax(B):
                        gi, dil, r_, n_, blk, ix = B
                        pp, PP5, PP4 = pps[ix % 2], PP5s[ix % 2], PP4s[ix % 2]
                        if n_ > 0:
                            for par in range(2):
                                ACT(PP5[:, :, :, par, :], bSc[par][:, :].rearrange("p (c j q) -> p c j q", c=2, j=2),
                                    AF.Exp, [bSc[par]], [pp], scale=0.125)
                            TT("pool", PP4, PP4,
                               mk[:].rearrange("p (c q) -> p c q", c=2).unsqueeze(2).to_broadcast([128, 2, 4, 128]),
                               ALU.mult, [pp, mk], [pp])
                        else:
                            for par in range(2):
                                ACT(PP5[:, 0, :, par, :], bSc[par][:, 0:256].rearrange("p (j q) -> p j q", j=2),
                                    AF.Exp, [bSc[par]], [pp], scale=0.125)
                            TT("pool", PP4[:, 0], PP4[:, 0],
                               mk[:, 0:128].unsqueeze(1).to_broadcast([128, 4, 128]), ALU.mult, [pp, mk], [pp])

                    def pv_acc(B):
                        gi, dil, r_, n_, blk, ix = B
                        bV = banks[5]
                        pp, PP4 = pps[ix % 2], PP4s[ix % 2]
                        kbs = ([(vaug[(ix - 1) % 6], 1)] if n_ > 0 else []) + [(vaug[ix % 6], 0)]
                        for j in range(2):
                            seq = [(vt, cp, h) for (vt, cp) in kbs for h in (2 * j, 2 * j + 1)]
                            for i_, (vt, cp, h) in enumerate(seq):
                                MM(bV[:, j * 128:(j + 1) * 128], vt[:, h * 128:(h + 1) * 128],
                                   PP4[:, cp, h, :], i_ == 0, i_ == len(seq) - 1, [vt, pp], [bV], signal=False)
                            for i_, (vt, cp, h) in enumerate(seq):
                                oz = onesE if h % 2 == 0 else onesO
                                MM(bV[:, 256 + j * 128:256 + (j + 1) * 128], oz[:],
                                   PP4[:, cp, h, :], i_ == 0, i_ == len(seq) - 1, [oz, pp], [bV],
                                   signal=(j == 1 and i_ == len(seq) - 1))
                        src = bV[:, :].rearrange("p (k q) -> p k q", k=4)
                        dst = tokv(B, AC4)
                        if gi == 0:
                            CP("dve", dst, src, [bV], [acc])
                        else:
                            TT("dve", dst, src, dst, ALU.add, [bV, acc], [acc])

                    nblk = len(blocks)
                    for k_ in range(3):
                        proj(blocks[k_])
                        evac_rope(blocks[k_])
                        if k_ == 0:
                            transp(blocks[0])
                    for i_b, B in enumerate(blocks):
                        scores(B)
                        softmax(B)
                        if i_b + 1 < nblk:
                            transp(blocks[i_b + 1])
                        if i_b >= 1:
                            pv_acc(blocks[i_b - 1])
                        if i_b + 3 < nblk:
                            proj(blocks[i_b + 3])
                            evac_rope(blocks[i_b + 3])
                    pv_acc(blocks[nblk - 1])
                    for hh in range(2):
                        dsl = slice(2 * S + hh * S, 2 * S + (hh + 1) * S)
                        ACT(acc[:, dsl], acc[:, dsl], AF.Ln, [acc], [acc])
                        ACT(acc[:, dsl], acc[:, dsl], AF.Exp, [acc], [acc], scale=-1.0)
                    TT("dve", attn_fm[:], acc[:, 0:2 * S], acc[:, 2 * S:4 * S], ALU.mult, [acc], [attn_fm])
                    if debug and q == 0:
                        DMA("sp", dbg_attn, attn_fm[:], dbg_slot, rd=[attn_fm])
                    P.barrier()
                    P.emit()
                    maybe_stop(f"{STAGE}{q}")
                sab.close()

                STAGE = "C"
                with ExitStack() as st:
                    Y = sb(st, "Y", [128, 2 * 8 * 512], BF16)
                    Y4 = Y[:].rearrange("p (mt s f) -> p mt s f", mt=2, s=8)
                    YI = Y[:].rearrange("p (mt g s c) -> p mt g s c", mt=2, g=32, s=8)
                    U = sb(st, "U", [128, 32 * 256], BF16)
                    U3 = U[:].rearrange("p (g m) -> p g m", g=32)
                    Xp = sb(st, "Xp", [128, 32 * 256], BF16)
                    X3 = Xp[:].rearrange("p (g m) -> p g m", g=32)
                    tabs2 = [[sb(st, f"tab{k}_{j}", [128, 1024], F32, dma=True) for k in range(3)] for j in range(2)]
                    m1 = sb(st, "m1", [128, 1024], F32)
                    m2 = sb(st, "m2", [128, 1024], F32)
                    zz = sb(st, "zz", [128, 1024], F32)
                    zs = sb(st, "zs", [128, 1024], F32)
                    HS = H3.rearrange("p k (mt i s) -> p k mt s i", mt=2, s=8)
                    for mt in range(2):
                        for s_ in range(8):
                            bk = banks[(mt * 8 + s_) % 2]
                            for kc in range(8):
                                MM(bk[:, :], HS[:, kc, mt, s_, :], WU3[:, kc, :], kc == 0, kc == 7,
                                   [h_fm, wus], [bk], signal=(kc == 7))
                            CP("act", YI[:, mt, :, s_, :], bk[:, :].rearrange("p (g c) -> p g c", g=32), [bk], [Y])
                    Y5 = Y[:].rearrange("p (mt s g c) -> p mt s g c", mt=2, s=8, g=32)
                    YF = Y[:].rearrange("p (mt g f) -> p mt g f", mt=2, g=32)
                    for g4 in range(8):
                        bk = banks[2 + (g4 % 2)]
                        bv = bfv(bk)
                        for gl in range(4):
                            g = g4 * 4 + gl
                            for mt in range(2):
                                o_ = (gl * 2 + mt) * 128
                                TR(bv[:, o_:o_ + 128], YF[:, mt, g, :], identb[:], [Y, identb], [bk],
                                   signal=(gl == 3 and mt == 1))
                        CP("dve", U[:, g4 * 1024:(g4 + 1) * 1024], bv[:, :], [bk], [U])
                    MSET("pool", X3[:, :, 0:1], 0.0, [Xp])
                    Yf = Y.t[:].bitcast(F32)
                    wsets = [(m1, m2, zz, zs),
                             tuple(T(Yf[:, k_ * 1024:(k_ + 1) * 1024], P.res(f"yalias{k_}")) for k_ in range(4))]
                    bG = [banks[4], banks[5]]
                    bS = [banks[6], banks[7]]

                    def c_tabs(g4n):
                        csn = slice(g4n * 1024, (g4n + 1) * 1024)
                        for k in range(3):
                            tn = tabs2[g4n % 2][k]
                            DMA("sp", tn[:], ssm_tab[k, :, csn], tn, rd=[r_tab], wr=[tn])

                    def c_first(g4):
                        tabs = tabs2[g4 % 2]
                        w1, w2, wz, wzs = wsets[g4 % 2]
                        for gl in range(4):
                            g = g4 * 4 + gl
                            MM(bG[gl // 2][:, (gl % 2) * 256:(gl % 2 + 1) * 256], MT4[:, 0, g, :], U3[:, g, :],
                               True, True, [mats, U], [bG[gl // 2]], signal=False)
                            MM(bS[gl // 2][:, (gl % 2) * 256:(gl % 2 + 1) * 256], MT4[:, 1, g, :], U3[:, g, :],
                               True, True, [mats, U], [bS[gl // 2]], signal=True)
                        for hb in range(2):
                            fs = slice(hb * 512, (hb + 1) * 512)
                            extra = [Y] if (g4 == 1 and hb == 0) else []
                            TT("dve", w1[:, fs], bG[hb][:, :], tabs[0][:, fs], ALU.mult, [bG[hb], tabs[0]], [w1] + extra)
                            TT("dve", w2[:, fs], bS[hb][:, :], tabs[1][:, fs], ALU.mult, [bS[hb], tabs[1]], [w2])
                        TT("dve", w1[:], w1[:], w2[:], ALU.add, [w1, w2], [w1])
                        P.op("dve", lambda e, tb_=tabs[2], wz_=wz, w1_=w1: e.tensor_tensor_scan(
                            out=wz_[:], data0=tb_[:], data1=w1_[:], initial=0.0, op0=ALU.mult, op1=ALU.add),
                            R([tabs[2], w1]), R([wz]))

                    def c_second(g4):
                        tabs = tabs2[g4 % 2]
                        w1, w2, wz, wzs = wsets[g4 % 2]
                        CP("act", wzs[0:64, :], wz[64:128, :], [wz], [wzs])
                        CP("act", wzs[64:128, :], wz[0:64, :], [wz], [wzs])
                        TT("dve", w1[:], wz[:], tabs[0][:], ALU.mult, [wz, tabs[0]], [w1])
                        TT("pool", w2[:], wzs[:], tabs[1][:], ALU.mult, [wzs, tabs[1]], [w2])
                        extra = [Y] if g4 == 7 else []
                        TT("dve", X3[:, g4 * 4:(g4 + 1) * 4, 1:256],
                           w1[:].rearrange("p (g m) -> p g m", g=4)[:, :, 0:255],
                           w2[:].rearrange("p (g m) -> p g m", g=4)[:, :, 0:255], ALU.subtract, [w1, w2], [Xp] + extra)

                    c_tabs(0)
                    c_tabs(1)
                    c_first(0)
                    for g4 in range(8):
                        if g4 + 1 < 8:
                            c_first(g4 + 1)
                        c_second(g4)
                        if g4 + 2 < 8:
                            c_tabs(g4 + 2)
                    for mt in range(2):
                        for g4 in range(8):
                            bk = banks[(g4 % 2)]
                            for gl in range(4):
                                g = g4 * 4 + gl
                                MM(bk[:, gl * 128:(gl + 1) * 128], U3[:, g, mt * 128:(mt + 1) * 128], MT4[:, 2, g, :],
                                   True, False, [U, mats], [bk], signal=False)
                                MM(bk[:, gl * 128:(gl + 1) * 128], X3[:, g, mt * 128:(mt + 1) * 128], MT4[:, 3, g, :],
                                   False, True, [Xp, mats], [bk], signal=(gl == 3))
                            ACT(Y5[:, mt, :, g4 * 4:(g4 + 1) * 4, :].rearrange("p s g c -> p g s c"),
                                bk[:, :].rearrange("p (g s c) -> p g s c", g=4, s=8), AF.Gelu, [bk], [Y])
                    for mt in range(2):
                        for ch in range(4):
                            bk = banks[2 + ((mt * 4 + ch) % 2)]
                            bv = bfv(bk)
                            for s_ in range(8):
                                TR(bv[:, s_ * 128:(s_ + 1) * 128], Y4[:, mt, s_, ch * 128:(ch + 1) * 128], identb[:],
                                   [Y, identb], [bk], signal=(s_ == 7))
                            CP("dve", YA3[:, ch, mt * 1024:(mt + 1) * 1024].rearrange("p (m s) -> p s m", s=8),
                               bv.rearrange("p (s m) -> p s m", s=8), [bk], [yact_fm])
                    if debug and q == 0:
                        DMA("sp", dbg_y, yact_fm[:], dbg_slot, rd=[yact_fm])
                    P.barrier()
                    P.emit()
                    maybe_stop(f"{STAGE}{q}")
                sseq.close()

                STAGE = "D"
                with ExitStack() as st:
                    wgt = sb(st, "wgt", [128, 8 * 2048], BF16, dma=True)
                    WG3 = wgt[:].rearrange("p (k c) -> p k c", k=8)
                    DMA("pool", WG3[:, 0:4, :], wgate[0:512].rearrange("(k p) c -> p k c", p=128), wgt, wr=[wgt])
                    DMA("pool", WG3[:, 4:8, :], wgate[512:1024].rearrange("(k p) c -> p k c", p=128), wgt, wr=[wgt])
                    wgl = sb(st, "wgl", [128, 4 * 2048], BF16, dma=True)
                    WL3 = wgl[:].rearrange("p (k c) -> p k c", k=4)
                    DMA("pool", WL3, wglu.rearrange("(k p) c -> p k c", p=128), wgl, wr=[wgl])
                    waos = sb(st, "waos", [128, 2 * D], BF16, dma=True)
                    WA3 = waos[:].rearrange("p (k c) -> p k c", k=2)
                    DMA("pool", WA3, wao.rearrange("(k p) c -> p k c", p=128), waos, wr=[waos])
                    wos = sb(st, "wos", [128, 8 * D], BF16, dma=True)
                    WO3 = wos[:].rearrange("p (k c) -> p k c", k=8)
                    DMA("pool", WO3, wout.rearrange("(k p) c -> p k c", p=128), wos, wr=[wos])
                    mrg = sb(st, "mrg", [128, 8 * 512], BF16)
                    MG3 = mrg[:].rearrange("p (k t) -> p k t", k=8)
                    sA = sb(st, "sA", [128, 512], F32)
                    sB = sb(st, "sB", [128, 512], F32)
                    sE = sb(st, "sE", [128, 512], F32)
                    xst = [sb(st, f"xsd{i}", [128, D], F32, dma=True) for i in range(2)]
                    x1s = [sb(st, f"x1s{i}", [128, D], F32, dma=True) for i in range(2)]
                    for tb in range(4):
                        tsl = slice(tb * 512, (tb + 1) * 512)
                        for dc in range(8):
                            par = dc % 2
                            bG0, bG1, bAD = banks[0], banks[1], banks[2]
                            bZA, bZB = banks[3], banks[4]
                            c0 = slice(dc * 128, (dc + 1) * 128)
                            c1 = slice(1024 + dc * 128, 1024 + (dc + 1) * 128)
                            for kc in range(8):
                                MM(bG0[:, :], WG3[:, kc, c0], H3[:, kc, tsl], kc == 0, kc == 7, [wgt, h_fm], [bG0], signal=(kc == 7))
                            for kc in range(8):
                                MM(bG1[:, :], WG3[:, kc, c1], H3[:, kc, tsl], kc == 0, kc == 7, [wgt, h_fm], [bG1], signal=(kc == 7))
                            for kc in range(2):
                                MM(bAD[:, :], WA3[:, kc, c0], AT3[:, kc, tsl], kc == 0, kc == 1, [waos, attn_fm], [bAD], signal=(kc == 1))
                            for kc in range(4):
                                MM(bZA[:, :], WL3[:, kc, c0], YA3[:, kc, tsl], kc == 0, kc == 3, [wgl, yact_fm], [bZA], signal=(kc == 3))
                            for kc in range(4):
                                MM(bZB[:, :], WL3[:, kc, c1], YA3[:, kc, tsl], kc == 0, kc == 3, [wgl, yact_fm], [bZB], signal=(kc == 3))
                            ACT(sA[:], bG0[:, :], AF.Sigmoid, [bG0], [sA])
                            ACT(sB[:], bG1[:, :], AF.Sigmoid, [bG1], [sB])
                            ACT(sE[:], bZB[:, :], AF.Sigmoid, [bZB], [sE])
                            TT("dve", sA[:], sA[:], bAD[:, :], ALU.mult, [sA, bAD], [sA])
                            TT("dve", sE[:], sE[:], bZA[:, :], ALU.mult, [sE, bZA], [sE])
                            TT("dve", sE[:], sE[:], sB[:], ALU.mult, [sE, sB], [sE])
                            TT("dve", MG3[:, dc, :], sA[:], sE[:], ALU.add, [sA, sE], [mrg])
                        for i in range(4):
                            tt = tb * 4 + i
                            xs = xst[tt % 2]
                            x1 = x1s[tt % 2]
                            rows = slice(row0 + tt * 128, row0 + (tt + 1) * 128)
                            DMA("sp", xs[:], x[rows, :], xs, wr=[xs])
                            bO = [banks[5], banks[6]]
                            for half in range(2):
                                for kc in range(8):
                                    MM(bO[half][:, :], MG3[:, kc, i * 128:(i + 1) * 128], WO3[:, kc, half * 512:(half + 1) * 512],
                                       kc == 0, kc == 7, [mrg, wos], [bO[half]], signal=(kc == 7))
                            for half in range(2):
                                hs = slice(half * 512, (half + 1) * 512)
                                TT("dve", x1[:, hs], bO[half][:, :], xs[:, hs], ALU.add, [bO[half], xs], [x1])
                            DMA("sp", x1buf[rows, :], x1[:], x1, rd=[x1], wr=[r_x1])
                    P.barrier()
                    P.emit()
                    maybe_stop(f"{STAGE}{q}")

        with ExitStack() as st:
            wgs = sb(st, "wgs", [128, 8 * DFF], BF16, dma=True)
            wus2 = sb(st, "wus2", [128, 8 * DFF], BF16, dma=True)
            wds = sb(st, "wds", [128, NFC * D], BF16, dma=True)
            WGF = wgs[:].rearrange("p (k c) -> p k c", k=8)
            WUF = wus2[:].rearrange("p (k c) -> p k c", k=8)
            WDF = wds[:].rearrange("p (k c) -> p k c", k=NFC)
            for kc in range(8):
                DMA("pool", WGF[:, kc, :], wfg[kc * 128:(kc + 1) * 128, :], wgs, wr=[wgs])
                DMA("pool", WUF[:, kc, :], wfu[kc * 128:(kc + 1) * 128, :], wus2, wr=[wus2])
            for c2 in range(0, NFC, 2):
                DMA("pool", WDF[:, c2:c2 + 2, :], wfd[c2 * 128:(c2 + 2) * 128, :].rearrange("(k p) c -> p k c", p=128), wds, wr=[wds])
            cut("f0")
            x1t = [sb(st, f"x1t{i}", [128, D], F32, dma=True) for i in range(4)]
            h2tm = [sb(st, f"h2tm{i}", [128, D], BF16) for i in range(2)]
            h2fm = sb(st, "h2fm", [128, 8 * 512], BF16)
            H2 = h2fm[:].rearrange("p (k t) -> p k t", k=8)
            actb = sb(st, "actb", [128, NFC * 512], BF16)
            AC3 = actb[:].rearrange("p (k t) -> p k t", k=NFC)
            sg = [sb(st, f"sg{i}", [128, 512], F32) for i in range(1)]
            junk2_ = sb(st, "junk2", [128, D], BF16)
            wks = [(junk2_, sb(st, f"ss_{i}", [128, 1], F32), sb(st, f"ss2_{i}", [128, 1], F32),
                    sb(st, f"rstd_{i}", [128, 1], F32)) for i in range(4)]
            NB = NSEQ * S // 512
            xf = x1t[0:2]
            xr = x1t[2:4]
            wks_f, wks_b = wks[0:2], wks[2:4]

            def rows_of(blk, i):
                tt = blk * 4 + i
                return slice(tt * 128, (tt + 1) * 128)

            def ff_load(blk, i):
                DMA("sp", xf[i % 2][:], x1buf[rows_of(blk, i), :], xf[i % 2], rd=[r_x1], wr=[xf[i % 2]])

            def ff_norm(blk, i):
                rmsnorm_tile(xf[i % 2], 1, h2tm[i % 2], wks_f[i % 2])

            def ff_tr(blk, i):
                ht = h2tm[i % 2]
                bk = banks[6 + (i % 2)]
                bv = bfv(bk)
                for kc in range(8):
                    TR(bv[:, kc * 128:(kc + 1) * 128], ht[:, kc * 128:(kc + 1) * 128], identb[:],
                       [ht, identb], [bk], signal=(kc == 7))
                CP("act", H2[:, :, i * 128:(i + 1) * 128], bv.rearrange("p (k t) -> p k t", k=8), [bk], [h2fm])

            def rr_load(blk, i):
                DMA("sp", xr[i % 2][:], x1buf[rows_of(blk, i), :], xr[i % 2], rd=[r_x1], wr=[xr[i % 2]])

            def down(blk, i):
                xt = xr[i % 2]
                bO = [banks[4], banks[5]]
                for half in range(2):
                    for c in range(NFC):
                        MM(bO[half][:, :], AC3[:, c, i * 128:(i + 1) * 128], WDF[:, c, half * 512:(half + 1) * 512],
                           c == 0, c == NFC - 1, [actb, wds], [bO[half]], signal=(c == NFC - 1))
                for half in range(2):
                    hs = slice(half * 512, (half + 1) * 512)
                    TT("dve", xt[:, hs], bO[half][:, :], xt[:, hs], ALU.add, [bO[half], xt], [xt])
                rmsnorm_tile(xt, 2, xt, wks_b[i % 2])
                out_toks.append(DMA("sp", out[rows_of(blk, i), :], xt[:], xt, rd=[xt]))

            ff_load(0, 0)
            ff_load(0, 1)
            ff_norm(0, 0)
            ff_load(0, 2)
            ff_norm(0, 1)
            ff_load(0, 3)
            for i in range(4):
                ff_tr(0, i)
                if i + 2 < 4:
                    ff_norm(0, i + 2)
            for blk in range(NB):
                nb_ = blk + 1 if blk + 1 < NB else None
                rr_load(blk, 0)
                rr_load(blk, 1)
                if nb_ is not None:
                    ff_load(nb_, 0)
                    ff_load(nb_, 1)
                for c in range(NFC):
                    bg = banks[c % 2]
                    bu = banks[2 + (c % 2)]
                    cs = slice(c * 128, (c + 1) * 128)
                    for kc in range(8):
                        MM(bg[:, :], WGF[:, kc, cs], H2[:, kc, :], kc == 0, kc == 7, [wgs, h2fm], [bg], signal=(kc == 7))
                    for kc in range(8):
                        MM(bu[:, :], WUF[:, kc, cs], H2[:, kc, :], kc == 0, kc == 7, [wus2, h2fm], [bu], signal=(kc == 7))
                    s_ = sg[0]
                    ACT(s_[:], bg[:, :], AF.Silu, [bg], [s_])
                    TT("dve", AC3[:, c, :], s_[:], bu[:, :], ALU.mult, [s_, bu], [actb])
                    if nb_ is not None and c == 17:
                        ff_norm(nb_, 0)
                        ff_load(nb_, 2)
                    if nb_ is not None and c == 19:
                        ff_norm(nb_, 1)
                        ff_load(nb_, 3)
                for i in range(4):
                    if nb_ is not None:
                        ff_tr(nb_, i)
                        if i + 2 < 4:
                            ff_norm(nb_, i + 2)
                    down(blk, i)
                    if i + 2 < 4:
                        rr_load(blk, i + 2)
            for t in out_toks[-2:]:
                P._wait("sp", t)
            P.barrier()
            P.emit()
    except _Stop:
        pass
    return nc


def _host_consts():
    bf = ml_dtypes.bfloat16
    c = {}
    c["c_ident"] = np.eye(128, dtype=np.float32)
    kk = np.arange(128)[:, None]
    qq = np.arange(128)[None, :]
    c["c_maskc"] = (kk <= qq).astype(np.float32).astype(bf)
    c["c_maskp"] = (kk >= qq).astype(np.float32).astype(bf)
    half = 8
    inv = np.power(np.float32(500000.0), -np.arange(half, dtype=np.float32) * np.float32(2.0) / np.float32(16.0)).astype(np.float32)
    rc = np.zeros((128, 3, 16, 8), np.float32)
    rs = np.zeros((128, 3, 16, 8), np.float32)
    i = np.arange(128)
    for gi, (_, dil) in enumerate(GROUPS):
        nb = (S // dil) // 128
        for r_ in range(dil):
            for n_ in range(nb):
                blk = r_ * nb + n_
                pos = (r_ + dil * (128 * n_ + i)).astype(np.float32)
                ang = pos[:, None] * inv[None, :]
                rc[:, gi, blk, :] = np.cos(ang)
                rs[:, gi, blk, :] = np.sin(ang)
    c["c_ropec"] = np.concatenate([rc, rc], axis=3).reshape(128, 768)
    c["c_ropes"] = np.concatenate([-rs, rs], axis=3).reshape(128, 768)
    jv = np.repeat(np.arange(-7, 16, dtype=np.float32)[None, :, None], 32, axis=2)
    c["c_jv"] = np.broadcast_to(jv, (128, NJ, 32)).reshape(128, NJ * 32).copy()
    c["c_mrow"] = np.broadcast_to(np.arange(256, dtype=np.float32)[None, :], (128, 256)).copy()
    st = np.ones(256, np.float32)
    st[0] = 0.0
    c["c_step"] = np.broadcast_to(st[None, :], (128, 256)).copy()
    s_i = (np.arange(128) // 16)[:, None]
    t_i = (np.arange(128) // 16)[None, :]
    c["c_tmask"] = (t_i >= s_i).astype(np.float32)
    return c


def _prep_inputs(inputs):
    f = lambda a: np.ascontiguousarray(np.asarray(a, dtype=np.float32))
    w_in = f(inputs["w_in"])[0]
    shared = {}
    wq = np.zeros((3, D, 768), np.float32)
    for gi in range(3):
        cols = []
        for t3 in range(3):
            for h in range(4):
                head = gi * 4 + h
                base = (t3 * 12 + head) * 64
                cols.append(np.arange(base, base + 64))
        wq[gi] = w_in[:, np.concatenate(cols)]
    shared["wqkv"] = wq
    shared["wu"] = f(w_in[:, 2304:2816])
    shared["wgate"] = f(w_in[:, 2816:4864])
    shared["wglu"] = f(inputs["w_glu"])[0]
    shared["wao"] = f(inputs["w_attn_out"])[0]
    shared["wout"] = f(inputs["w_out"])[0]
    shared["wfg"] = f(inputs["w_ffn_gate"])[0]
    shared["wfu"] = f(inputs["w_ffn_up"])[0]
    shared["wfd"] = f(inputs["w_ffn_down"])[0]
    shared["gains"] = np.stack([f(inputs["norm_mix_g"])[0], f(inputs["norm_ffn_g"])[0], f(inputs["norm_final_g"])], 0)
    shared["a_re"] = f(inputs["ssm_a_re"])[0]
    shared["a_im"] = f(inputs["ssm_a_im"])[0]
    shared["log_dt"] = f(inputs["ssm_log_dt"]).reshape(1, 32)
    shared["b_re"] = f(inputs["ssm_b_re"])[0]
    shared["b_im"] = f(inputs["ssm_b_im"])[0]
    shared["c_re"] = f(inputs["ssm_c_re"])[0].reshape(512, 64)
    shared["c_im"] = f(inputs["ssm_c_im"])[0].reshape(512, 64)
    shared["dsk"] = f(inputs["ssm_d"])[0]
    shared.update(_host_consts())
    xs = f(inputs["x"]).reshape(NCORES, NSEQ * S, D)
    return [dict(shared, x=xs[i]) for i in range(NCORES)]


def kernel(**inputs):
    in_maps = _prep_inputs(inputs)
    nc = build_program()
    res = run_bass_kernel_spmd(nc, in_maps, core_ids=list(range(NCORES)))
    outs = [np.asarray(r["out"], dtype=np.float32) for r in res.results]
    return np.concatenate(outs, axis=0).reshape(16, S, D)
```

```python
import math
from contextlib import ExitStack

import numpy as np
import ml_dtypes

import concourse.bass as bass
import concourse.mybir as mybir
from concourse.bass_utils import run_bass_kernel_spmd

F32 = mybir.dt.float32
BF16 = mybir.dt.bfloat16
AF = mybir.ActivationFunctionType
ALU = mybir.AluOpType

NCORES = 8
S = 2048
D = 1024
NSEQ = 2
DFF = 2816
NFC = DFF // 128
GROUPS = ((128, 1), (512, 4), (2048, 16))
TWO_PI = 2.0 * math.pi
MAGIC = 12582912.0
NJ = 23


class Res:
    __slots__ = ("name", "w", "r", "sem", "semv")

    def __init__(self, name, sem=None):
        self.name = name
        self.w = None
        self.r = []
        self.sem = sem
        self.semv = 0


class Prog:
    ENG = ("pe", "act", "dve", "pool", "sp")

    def __init__(self, nc, stack):
        self.nc = nc
        self.stack = stack
        self.sem = {e: stack.enter_context(nc.semaphore("s_" + e)) for e in self.ENG}
        self.cnt = {e: 0 for e in self.ENG}
        self.pending = {e: False for e in self.ENG}
        self.waited = {e: {} for e in self.ENG}
        self.ops = {e: [] for e in self.ENG}
        self.slots = []

    def res(self, name, dma=False):
        sem = None
        if dma:
            sem = self.stack.enter_context(self.nc.semaphore("d_" + name))
        r = Res(name, sem)
        if dma:
            self.slots.append(r)
        return r

    def _wait(self, eng, tok):
        sem, val = tok
        key = id(sem)
        if self.waited[eng].get(key, 0) >= val:
            return
        self.waited[eng][key] = val
        self.ops[eng].append(("wait", sem, val))

    def _deps(self, eng, reads, writes):
        pes = self.sem["pe"]
        for r in reads:
            if r.w is not None and not (eng == "pe" and r.w[0] is pes):
                self._wait(eng, r.w)
        for w in writes:
            if w.w is not None and not (eng == "pe" and w.w[0] is pes):
                self._wait(eng, w.w)
            for t in w.r:
                if not (eng == "pe" and t[0] is pes):
                    self._wait(eng, t)

    def op(self, eng, fn, reads=(), writes=(), signal=True):
        self._deps(eng, reads, writes)
        tok = (self.sem[eng], self.cnt[eng] + 1)
        if signal:
            self.cnt[eng] += 1
            self.pending[eng] = False
        else:
            self.pending[eng] = True
        self.ops[eng].append(("op", fn, signal))
        for r in reads:
            r.r.append(tok)
        for w in writes:
            w.w = tok
            w.r = []

    def dma(self, eng, out, in_, slot, reads=(), writes=()):
        self._deps(eng, reads, writes)
        slot.semv += 16
        tok = (slot.sem, slot.semv)
        self.ops[eng].append(("dma", out, in_, slot.sem))
        for r in reads:
            r.r.append(tok)
        for w in writes:
            w.w = tok
            w.r = []
        return tok

    def barrier(self, skip=()):
        skip_ids = {id(r) for r in skip}
        toks = [(self.sem[e], self.cnt[e]) for e in self.ENG if self.cnt[e] > 0]
        toks += [(s.sem, s.semv) for s in self.slots if s.semv > 0 and id(s) not in skip_ids]
        for e in self.ENG:
            assert not self.pending[e]
            for t in toks:
                if t[0] is not self.sem[e]:
                    self._wait(e, t)

    def emit(self):
        nc = self.nc
        for e in self.ENG:
            assert not self.pending[e], e
        if not any(self.ops[e] for e in self.ENG):
            return
        with nc.allow_non_contiguous_dma(reason="small strided setup loads"), nc.Block() as block:
            def run(engname, engobj):
                sem_self = self.sem[engname]
                for o in self.ops[engname]:
                    if o[0] == "wait":
                        engobj.wait_ge(o[1], o[2])
                    elif o[0] == "op":
                        ins = o[1](engobj)
                        if o[2]:
                            ins.then_inc(sem_self, 1)
                    else:
                        engobj.dma_start(out=o[1], in_=o[2]).then_inc(o[3], 16)

            @block.tensor
            def _(e):
                run("pe", e)

            @block.scalar
            def _(e):
                run("act", e)

            @block.vector
            def _(e):
                run("dve", e)

            @block.gpsimd
            def _(e):
                run("pool", e)

            @block.sync
            def _(e):
                run("sp", e)
        self.ops = {e: [] for e in self.ENG}


class T:
    def __init__(self, t, r):
        self.t = t
        self.r = r

    def __getitem__(self, k):
        return self.t[k]


class _Stop(Exception):
    pass


def build_program(debug=False, stop=None):
    nc = bass.Bass("TRN2", target_bir_lowering=False)

    def din(name, shape, dt=F32):
        return nc.dram_tensor(name, list(shape), dt, kind="ExternalInput").ap()

    x = din("x", [NSEQ * S, D])
    wqkv = din("wqkv", [3, D, 768])
    wu = din("wu", [D, 512])
    wgate = din("wgate", [D, 2048])
    wglu = din("wglu", [512, 2048])
    wao = din("wao", [256, D])
    wout = din("wout", [D, D])
    wfg = din("wfg", [D, DFF])
    wfu = din("wfu", [D, DFF])
    wfd = din("wfd", [DFF, D])
    gains = din("gains", [3, D])
    a_re = din("a_re", [32, 64])
    a_im = din("a_im", [32, 64])
    log_dt = din("log_dt", [1, 32])
    b_re = din("b_re", [32, 64, 16])
    b_im = din("b_im", [32, 64, 16])
    c_re = din("c_re", [512, 64])
    c_im = din("c_im", [512, 64])
    dsk = din("dsk", [32, 16])
    c_ident = din("c_ident", [128, 128])
    c_maskc = din("c_maskc", [128, 128], BF16)
    c_maskp = din("c_maskp", [128, 128], BF16)
    c_ropec = din("c_ropec", [128, 768])
    c_ropes = din("c_ropes", [128, 768])
    c_jv = din("c_jv", [128, NJ * 32])
    c_mrow = din("c_mrow", [128, 256])
    c_step = din("c_step", [128, 256])
    c_tmask = din("c_tmask", [128, 128])
    out = nc.dram_tensor("out", [NSEQ * S, D], F32, kind="ExternalOutput").ap()
    x1buf = nc.dram_tensor("x1buf", [NSEQ * S, D], F32, kind=("ExternalOutput" if debug else "Internal")).ap()
    ssm_mats = nc.dram_tensor("ssm_mats", [4, 128, 4096], BF16).ap()
    ssm_tab = nc.dram_tensor("ssm_tab", [3, 128, 8192], F32).ap()
    dbg_attn = dbg_y = None
    if debug:
        dbg_attn = nc.dram_tensor("dbg_attn", [128, 2 * S], BF16, kind="ExternalOutput").ap()
        dbg_y = nc.dram_tensor("dbg_y", [128, 4 * S], BF16, kind="ExternalOutput").ap()

    try:
      with ExitStack() as top:
        P = Prog(nc, top)

        def maybe_stop(tag):
            if stop == tag:
                raise _Stop()

        def cut(tag):
            if stop == tag:
                P.barrier()
                P.emit()
                raise _Stop()

        uid = [0]

        def sb(st, name, shape, dt, dma=False):
            uid[0] += 1
            name = f"{name}_{uid[0]}"
            t = st.enter_context(nc.sbuf_tensor(name, list(shape), dt))
            return T(t, P.res(name, dma=dma))

        def R(ts):
            return [t.r if isinstance(t, T) else t for t in ts]

        def MM(o, lhsT, rhs, start, stop, rd, wr, signal=True):
            P.op("pe", lambda e: e.matmul(o, lhsT=lhsT, rhs=rhs, start=start, stop=stop),
                 R(rd), R(wr), signal)

        def TR(o, in_, ident, rd, wr, signal=True):
            P.op("pe", lambda e: e.transpose(o, in_, ident), R(rd), R(wr), signal)

        def ACT(o, in_, func, rd, wr, scale=1.0, bias=0.0, accum_out=None):
            if accum_out is None:
                P.op("act", lambda e: e.activation(out=o, in_=in_, func=func, bias=bias, scale=scale),
                     R(rd), R(wr))
            else:
                P.op("act", lambda e: e.activation(out=o, in_=in_, func=func, bias=bias, scale=scale,
                                                   accum_out=accum_out), R(rd), R(wr))

        def CP(eng, o, in_, rd, wr):
            if eng == "act":
                ACT(o, in_, AF.Copy, rd, wr)
            else:
                P.op(eng, lambda e: e.tensor_copy(out=o, in_=in_), R(rd), R(wr))

        def TT(eng, o, in0, in1, op, rd, wr):
            P.op(eng, lambda e: e.tensor_tensor(out=o, in0=in0, in1=in1, op=op), R(rd), R(wr))

        def TS(eng, o, in0, s1, s2, op0, op1, rd, wr):
            if s2 is None:
                P.op(eng, lambda e: e.tensor_scalar(out=o, in0=in0, scalar1=s1, scalar2=None, op0=op0),
                     R(rd), R(wr))
            else:
                P.op(eng, lambda e: e.tensor_scalar(out=o, in0=in0, scalar1=s1, scalar2=s2, op0=op0, op1=op1),
                     R(rd), R(wr))

        def STT(eng, o, in0, sc, in1, op0, op1, rd, wr):
            P.op(eng, lambda e: e.scalar_tensor_tensor(out=o, in0=in0, scalar=sc, in1=in1, op0=op0, op1=op1),
                 R(rd), R(wr))

        def MSET(eng, o, val, wr):
            P.op(eng, lambda e: e.memset(o, val), [], R(wr))

        def RECIP(o, in_, rd, wr):
            P.op("dve", lambda e: e.reciprocal(out=o, in_=in_), R(rd), R(wr))

        def DMA(eng, o, in_, slot, rd=(), wr=()):
            return P.dma(eng, o, in_, slot.r if isinstance(slot, T) else slot, R(rd), R(wr))

        banks = []
        for i in range(8):
            t = top.enter_context(nc.psum_tensor(f"bank{i}", [128, 512], F32))
            banks.append(T(t, P.res(f"bank{i}")))

        def bfv(bank):
            return bank.t[:].bitcast(BF16)

        ident32 = sb(top, "ident32", [128, 128], F32, dma=True)
        identb = sb(top, "identb", [128, 128], BF16)
        maskc = sb(top, "maskc", [128, 128], BF16, dma=True)
        maskp = sb(top, "maskp", [128, 128], BF16, dma=True)
        ropec = sb(top, "ropec", [128, 768], F32, dma=True)
        ropes = sb(top, "ropes", [128, 768], F32, dma=True)
        gain = sb(top, "gain", [128, 3 * D], F32, dma=True)
        DMA("sp", ident32[:], c_ident, ident32, wr=[ident32])
        DMA("sp", maskc[:], c_maskc, maskc, wr=[maskc])
        DMA("sp", maskp[:], c_maskp, maskp, wr=[maskp])
        DMA("sp", ropec[:], c_ropec, ropec, wr=[ropec])
        DMA("sp", ropes[:], c_ropes, ropes, wr=[ropes])
        for i in range(3):
            DMA("sp", gain[:, i * D:(i + 1) * D], gains[i:i + 1, :].partition_broadcast(128), gain, wr=[gain])
        CP("dve", identb[:], ident32[:], [ident32], [identb])
        r_mats = P.res("ssm_mats", dma=True)
        r_tab = P.res("ssm_tab", dma=True)
        r_x1 = P.res("x1buf", dma=True)
        dbg_slot = P.res("dbg", dma=True)
        out_toks = []

        def rmsnorm_tile(xt, gidx, h_out, wk, rd_extra=()):
            junk, ss, ss2, rstd = wk
            MSET("dve", ss[:, 0:1], 0.0, [ss])
            ACT(junk[:], xt[:], AF.Square, [xt], [junk, ss], accum_out=ss[:, 0:1])
            TS("dve", ss2[:, 0:1], ss[:, 0:1], 1.0 / D, 1e-6, ALU.mult, ALU.add, [ss], [ss2])
            ACT(ss2[:, 0:1], ss2[:, 0:1], AF.Sqrt, [ss2], [ss2])
            RECIP(rstd[:, 0:1], ss2[:, 0:1], [ss2], [rstd])
            STT("dve", h_out[:], xt[:], rstd[:, 0:1], gain[:, gidx * D:(gidx + 1) * D], ALU.mult, ALU.mult,
                [xt, rstd, gain], [h_out])

        with ExitStack() as st:
            A1 = sb(st, "A1", [32, 64], F32, dma=True)
            A2 = sb(st, "A2", [32, 64], F32, dma=True)
            DMA("sp", A1[:], a_re, A1, wr=[A1])
            DMA("sp", A2[:], a_im, A2, wr=[A2])
            lr = sb(st, "lr", [128, 32], F32)
            li = sb(st, "li", [128, 32], F32)
            dt_ = sb(st, "dt_", [128, 32], F32, dma=True)
            DMA("sp", dt_[:], log_dt.partition_broadcast(128), dt_, wr=[dt_])
            for (src, dst, bk) in ((A1, lr, banks[0]), (A2, li, banks[1])):
                TR(bk[0:64, 0:32], src[:], ident32[0:32, 0:32], [src, ident32], [bk])
                CP("act", dst[0:64, :], bk[0:64, 0:32], [bk], [dst])
                CP("act", dst[64:128, :], bk[0:64, 0:32], [bk], [dst])
            cut("s1")
            ACT(dt_[:], dt_[:], AF.Exp, [dt_], [dt_])
            lrdt = sb(st, "lrdt", [128, 32], F32)
            th = sb(st, "th", [128, 32], F32)
            TT("dve", lrdt[:], lr[:], dt_[:], ALU.mult, [lr, dt_], [lrdt])
            TT("dve", th[:], li[:], dt_[:], ALU.mult, [li, dt_], [th])
            jv = sb(st, "jv", [128, NJ * 32], F32, dma=True)
            DMA("sp", jv[:], c_jv, jv, wr=[jv])
            jv3 = jv[:].rearrange("p (j g) -> p j g", j=NJ)
            PR = sb(st, "PR", [128, NJ * 32], F32)
            PI = sb(st, "PI", [128, NJ * 32], F32)
            MAG = sb(st, "MAG", [128, NJ * 32], F32)
            ANG = sb(st, "ANG", [128, NJ * 32], F32)
            RT = sb(st, "RT", [128, NJ * 32], F32)

            def v3(t):
                return t[:].rearrange("p (j g) -> p j g", j=NJ)

            def bc_g(t):
                return t[:].unsqueeze(1).to_broadcast([128, NJ, 32])

            def range_reduce(dst, src, tmp, shift, n):
                TS("dve", tmp[:, 0:n], src[:, 0:n], shift, 1.0 / TWO_PI, ALU.add, ALU.mult, [src], [tmp])
                TS("dve", tmp[:, 0:n], tmp[:, 0:n], MAGIC, -MAGIC, ALU.add, ALU.add, [tmp], [tmp])
                TS("dve", tmp[:, 0:n], tmp[:, 0:n], -TWO_PI, shift, ALU.mult, ALU.add, [tmp], [tmp])
                TT("dve", dst[:, 0:n], tmp[:, 0:n], src[:, 0:n], ALU.add, [tmp, src], [dst])
                TS("dve", dst[:, 0:n], dst[:, 0:n], 3.14159, -3.14159, ALU.min, ALU.max, [dst], [dst])

            TT("dve", v3(MAG), jv3, bc_g(lrdt), ALU.mult, [jv, lrdt], [MAG])
            ACT(MAG[:], MAG[:], AF.Exp, [MAG], [MAG])
            TT("dve", v3(ANG), jv3, bc_g(th), ALU.mult, [jv, th], [ANG])
            n_all = NJ * 32
            range_reduce(PI, ANG, RT, 0.0, n_all)
            ACT(PI[:], PI[:], AF.Sin, [PI], [PI])
            range_reduce(PR, ANG, RT, math.pi / 2, n_all)
            ACT(PR[:], PR[:], AF.Sin, [PR], [PR])
            TT("dve", PR[:], PR[:], MAG[:], ALU.mult, [PR, MAG], [PR])
            TT("dve", PI[:], PI[:], MAG[:], ALU.mult, [PI, MAG], [PI])

            def pj(t, e):
                return t[:, (e + 7) * 32:(e + 8) * 32]

            cut("s2")
            f_re = sb(st, "f_re", [128, 32], F32)
            f_im = sb(st, "f_im", [128, 32], F32)
            w1 = sb(st, "w1", [128, 32], F32)
            w2 = sb(st, "w2", [128, 32], F32)
            w3 = sb(st, "w3", [128, 32], F32)
            nr = sb(st, "nr", [128, 32], F32)
            TS("dve", nr[:], pj(PR, 1), -1.0, None, ALU.add, None, [PR], [nr])
            TT("dve", w1[:], lr[:], lr[:], ALU.mult, [lr], [w1])
            TT("dve", w2[:], li[:], li[:], ALU.mult, [li], [w2])
            TT("dve", w1[:], w1[:], w2[:], ALU.add, [w1, w2], [w1])
            RECIP(w3[:], w1[:], [w1], [w3])
            TT("dve", w1[:], nr[:], lr[:], ALU.mult, [nr, lr], [w1])
            TT("dve", w2[:], pj(PI, 1), li[:], ALU.mult, [PI, li], [w2])
            TT("dve", w1[:], w1[:], w2[:], ALU.add, [w1, w2], [w1])
            TT("dve", f_re[:], w1[:], w3[:], ALU.mult, [w1, w3], [f_re])
            TT("dve", w1[:], pj(PI, 1), lr[:], ALU.mult, [PI, lr], [w1])
            TT("dve", w2[:], nr[:], li[:], ALU.mult, [nr, li], [w2])
            TT("dve", w1[:], w1[:], w2[:], ALU.subtract, [w1, w2], [w1])
            TT("dve", f_im[:], w1[:], w3[:], ALU.mult, [w1, w3], [f_im])

            cut("s3")
            Br = sb(st, "Br", [128, 512], F32, dma=True)
            Bi = sb(st, "Bi", [128, 512], F32, dma=True)
            for (src, dst) in ((b_re, Br), (b_im, Bi)):
                for half in range(2):
                    for gq in range(4):
                        DMA("sp", dst[half * 64:(half + 1) * 64, gq * 128:(gq + 1) * 128].rearrange("n (g c) -> n g c", g=8),
                            src[gq * 8:(gq + 1) * 8].rearrange("g n c -> n g c"), dst, wr=[dst])

            def g16(t):
                return t[:].rearrange("p (g c) -> p g c", g=32)

            def bc16(ap):
                return ap.unsqueeze(2).to_broadcast([128, 32, 16])

            BX = sb(st, "BX", [128, 512], F32)
            BY = sb(st, "BY", [128, 512], F32)
            t5 = sb(st, "t5", [128, 512], F32)
            t6 = sb(st, "t6", [128, 512], F32)
            Bbr = sb(st, "Bbr", [128, 512], F32)
            Bbi = sb(st, "Bbi", [128, 512], F32)
            TT("dve", g16(t5), g16(Br), bc16(f_re[:]), ALU.mult, [Br, f_re], [t5])
            TT("dve", g16(t6), g16(Bi), bc16(f_im[:]), ALU.mult, [Bi, f_im], [t6])
            TT("dve", Bbr[:], t5[:], t6[:], ALU.subtract, [t5, t6], [Bbr])
            TT("dve", g16(t5), g16(Bi), bc16(f_re[:]), ALU.mult, [Bi, f_re], [t5])
            TT("dve", g16(t6), g16(Br), bc16(f_im[:]), ALU.mult, [Br, f_im], [t6])
            TT("dve", Bbi[:], t5[:], t6[:], ALU.add, [t5, t6], [Bbi])
            CP("dve", BX[0:64, :], Bbr[0:64, :], [Bbr], [BX])
            CP("dve", BX[64:128, :], Bbi[64:128, :], [Bbi], [BX])
            TS("dve", BY[0:64, :], Bbi[0:64, :], -1.0, None, ALU.mult, None, [Bbi], [BY])
            CP("dve", BY[64:128, :], Bbr[64:128, :], [Bbr], [BY])

            cut("s4")
            CX = sb(st, "CX", [128, 512], F32)
            CY = sb(st, "CY", [128, 512], F32)
            cst = [sb(st, f"cst{i}", [128, 64], F32, dma=True) for i in range(2)]
            k = 0
            for (src, which) in ((c_re, 0), (c_im, 1)):
                for q4 in range(4):
                    stg = cst[k % 2]
                    bk = banks[2 + (k % 2)]
                    k += 1
                    DMA("sp", stg[:], src[q4 * 128:(q4 + 1) * 128, :], stg, wr=[stg])
                    TR(bk[0:64, 0:128], stg[:], ident32[:], [stg, ident32], [bk])
                    cs = slice(q4 * 128, (q4 + 1) * 128)
                    if which == 0:
                        CP("act", CX[0:64, cs], bk[0:64, 0:128], [bk], [CX])
                        ACT(CY[64:128, cs], bk[0:64, 0:128], AF.Copy, [bk], [CY], scale=-1.0)
                    else:
                        ACT(CX[64:128, cs], bk[0:64, 0:128], AF.Copy, [bk], [CX], scale=-1.0)
                        ACT(CY[0:64, cs], bk[0:64, 0:128], AF.Copy, [bk], [CY], scale=-1.0)

            cut("s5")
            stm = ExitStack()
            Q = sb(stm, "Q", [128, 4096], F32)
            t5e = sb(stm, "t5e", [128, 4096], F32)
            EEx = sb(stm, "EEx", [128, 8192], F32)
            Q4 = Q[:].rearrange("p (g s c) -> p g s c", g=32, s=8)
            E4 = EEx[:].rearrange("p (g e c) -> p g e c", g=32, e=16)
            for s_ in range(8):
                TT("dve", g16(t5), g16(BX), bc16(pj(PR, -s_)), ALU.mult, [BX, PR], [t5])
                TT("dve", g16(t6), g16(BY), bc16(pj(PI, -s_)), ALU.mult, [BY, PI], [t6])
                TT("dve", Q4[:, :, s_, :], g16(t5), g16(t6), ALU.add, [t5, t6], [Q])
            for e_ in range(16):
                TT("dve", g16(t5), g16(CX), bc16(pj(PR, e_)), ALU.mult, [CX, PR], [t5])
                TT("dve", g16(t6), g16(CY), bc16(pj(PI, e_)), ALU.mult, [CY, PI], [t6])
                TT("dve", E4[:, :, e_, :], g16(t5), g16(t6), ALU.add, [t5, t6], [EEx])
            Qhi = sb(stm, "Qhi", [128, 4096], BF16)
            Qlo = sb(stm, "Qlo", [128, 4096], BF16)
            Ehi = sb(stm, "Ehi", [128, 4096], BF16)
            Elo = sb(stm, "Elo", [128, 4096], BF16)
            E0v = E4[:, :, 0:8, :]
            Ehv = Ehi[:].rearrange("p (g e c) -> p g e c", g=32, e=8)
            Elv = Elo[:].rearrange("p (g e c) -> p g e c", g=32, e=8)
            CP("act", Qhi[:], Q[:], [Q], [Qhi])
            TT("dve", t5e[:], Q[:], Qhi[:], ALU.subtract, [Q, Qhi], [t5e])
            CP("act", Qlo[:], t5e[:], [t5e], [Qlo])
            CP("act", Ehv, E0v, [EEx], [Ehi])
            TT("dve", t5e[:].rearrange("p (g e c) -> p g e c", g=32, e=8), E0v, Ehv, ALU.subtract, [EEx, Ehi], [t5e])
            CP("act", Elo[:], t5e[:], [t5e], [Elo])

            cut("s6")
            dnat = sb(stm, "dnat", [32, 16], F32, dma=True)
            DMA("sp", dnat[:], dsk, dnat, wr=[dnat])
            drep = sb(stm, "drep", [32, 128], F32)
            CP("dve", drep[:].rearrange("g (s c) -> g s c", s=8), dnat[:].unsqueeze(1).to_broadcast([32, 8, 16]), [dnat], [drep])
            Dcol = sb(stm, "Dcol", [128, 32], F32)
            TR(banks[3][:, 0:32], drep[:], ident32[0:32, 0:32], [drep, ident32], [banks[3]])
            CP("act", Dcol[:], banks[3][:, 0:32], [banks[3]], [Dcol])
            tmask = sb(stm, "tmask", [128, 128], F32, dma=True)
            DMA("sp", tmask[:], c_tmask, tmask, wr=[tmask])

            cut("s6b")
            matsb = sb(stm, "matsb", [128, 4 * 4096], BF16)
            M4 = matsb[:].rearrange("p (k g c) -> p k g c", k=4, g=32)
            mres = [T(matsb.t, P.res(f"matsb_k{i}")) for i in range(4)]
            tmpm_l = [sb(stm, f"tmpm{i}", [128, 128], F32) for i in range(2)]
            tmpm0_l = [sb(stm, f"tmpm0{i}", [128, 128], F32) for i in range(2)]
            for g in range(32):
                gs_ = slice(g * 128, (g + 1) * 128)
                bk = banks[4 + (g % 2)]
                tmpm, tmpm0 = tmpm_l[g % 2], tmpm0_l[g % 2]
                bkv = bfv(bk)
                TR(bkv[:, 0:128], Qhi[:, gs_], identb[:], [Qhi, identb], [bk], signal=False)
                MM(bk[:, 256:384], Qhi[:, gs_], Ehi[:, gs_], True, False, [Qhi, Ehi], [bk], signal=False)
                MM(bk[:, 256:384], Qlo[:, gs_], Ehi[:, gs_], False, False, [Qlo, Ehi], [bk], signal=False)
                MM(bk[:, 256:384], Qhi[:, gs_], Elo[:, gs_], False, True, [Qhi, Elo], [bk])
                CP("act", M4[:, 0, g, :], bkv[:, 0:128], [bk], [mres[0]])
                CP("act", M4[:, 1, g, 0:64], bkv[:, 64:128], [bk], [mres[1]])
                CP("act", M4[:, 1, g, 64:128], bkv[:, 0:64], [bk], [mres[1]])
                if g == 0:
                    cut("s6c")
                CP("act", tmpm0[:], bk[:, 256:384], [bk], [tmpm0])
                TT("dve", tmpm[:], tmpm0[:], tmask[:], ALU.mult, [tmpm0, tmask], [tmpm])
                if g == 0:
                    cut("s6d")
                STT("dve", M4[:, 2, g, :], ident32[:], Dcol[:, g:g + 1], tmpm[:], ALU.mult, ALU.add,
                    [ident32, Dcol, tmpm], [mres[2]])
            cut("s7")
            CP("act", M4[:, 3, :, :].rearrange("p g (e c) -> p g e c", e=8), E4[:, :, 8:16, :], [EEx], [mres[3]])
            for k in range(4):
                DMA("sp", ssm_mats[k], matsb[:, k * 4096:(k + 1) * 4096], r_mats, rd=[mres[k]], wr=[r_mats])

            P.barrier()
            P.emit()
            stm.close()
            cut("s8")
            phi = sb(st, "phi", [128, 32], F32)
            r8 = sb(st, "r8", [128, 32], F32)
            TS("dve", phi[:], th[:], 8.0, None, ALU.mult, None, [th], [phi])
            range_reduce(phi, phi, w1, 0.0, 32)
            TS("dve", r8[:], lrdt[:], 8.0, None, ALU.mult, None, [lrdt], [r8])
            ACT(r8[:], r8[:], AF.Exp, [r8], [r8])
            mrow = sb(st, "mrow", [128, 256], F32, dma=True)
            step = sb(st, "step", [128, 256], F32, dma=True)
            DMA("sp", mrow[:], c_mrow, mrow, wr=[mrow])
            DMA("sp", step[:], c_step, step, wr=[step])
            angb = sb(st, "angb", [128, 2048], F32)
            halfpi = sb(st, "halfpi", [128, 1], F32)
            MSET("dve", halfpi[:], math.pi / 2, [halfpi])
            tb1 = sb(st, "tb1", [128, 2048], F32)
            tbo = [sb(st, f"tbo{i}", [128, 2048], F32) for i in range(3)]
            for gb in range(4):
                gs = slice(gb * 8, (gb + 1) * 8)
                phb = phi[:, gs].unsqueeze(2).to_broadcast([128, 8, 256])
                r8b = r8[:, gs].unsqueeze(2).to_broadcast([128, 8, 256])
                mb = mrow[:].unsqueeze(1).to_broadcast([128, 8, 256])
                stb = step[:].unsqueeze(1).to_broadcast([128, 8, 256])

                def v8(t):
                    return t[:].rearrange("p (g m) -> p g m", g=8)
                TT("dve", v8(angb), phb, mb, ALU.mult, [phi, mrow], [angb])

                def rr3(dst, shift):
                    TS("dve", tb1[:], angb[:], shift, 1.0 / TWO_PI, ALU.add, ALU.mult, [angb], [tb1])
                    TS("dve", tb1[:], tb1[:], MAGIC, -MAGIC, ALU.add, ALU.add, [tb1], [tb1])
                    STT("dve", dst[:], tb1[:], -TWO_PI, angb[:], ALU.mult, ALU.add, [tb1, angb], [dst])
                rr3(tbo[0], math.pi / 2)
                ACT(tbo[0][:], tbo[0][:], AF.Sin, [tbo[0]], [tbo[0]], bias=halfpi[:, 0:1])
                rr3(tbo[1], 0.0)
                ACT(tbo[1][0:64, :], tbo[1][0:64, :], AF.Sin, [tbo[1]], [tbo[1]])
                ACT(tbo[1][64:128, :], tbo[1][64:128, :], AF.Sin, [tbo[1]], [tbo[1]], scale=-1.0)
                TT("dve", v8(tbo[2]), r8b, stb, ALU.mult, [r8, step], [tbo[2]])
                for k in range(3):
                    DMA("sp", ssm_tab[k, :, gb * 2048:(gb + 1) * 2048], tbo[k][:], r_tab, rd=[tbo[k]], wr=[r_tab])
            P.barrier()
            P.emit()
            maybe_stop("setup")

        with ExitStack() as mix:
            h_fm = sb(mix, "h_fm", [128, 8 * S], BF16)
            attn_fm = sb(mix, "attn_fm", [128, 2 * S], BF16)
            yact_fm = sb(mix, "yact_fm", [128, 4 * S], BF16)
            H3 = h_fm[:].rearrange("p (k t) -> p k t", k=8)
            AT3 = attn_fm[:].rearrange("p (j t) -> p j t", j=2)
            YA3 = yact_fm[:].rearrange("p (k t) -> p k t", k=4)

            for q in range(NSEQ):
                row0 = q * S
                sseq = mix.enter_context(ExitStack())
                wus = sb(sseq, "wus", [128, 8 * 512], BF16, dma=True)
                WU3 = wus[:].rearrange("p (k c) -> p k c", k=8)
                mats = sb(sseq, "mats", [128, 4 * 4096], BF16, dma=True)
                MT4 = mats[:].rearrange("p (k g c) -> p k g c", k=4, g=32)
                sab = mix.enter_context(ExitStack())
                wg_sb = [sb(sab, f"wg{i}", [128, 8 * 768], BF16, dma=True) for i in range(3)]
                W3s = []
                for gi in range(3):
                    wg = wg_sb[gi]
                    W3 = wg[:].rearrange("p (k c) -> p k c", k=8)
                    DMA("pool", W3, wqkv[gi].rearrange("(k p) c -> p k c", p=128), wg, wr=[wg])
                    W3s.append(W3)
                DMA("pool", WU3, wu.rearrange("(k p) c -> p k c", p=128), wus, wr=[wus])
                for k in range(4):
                    DMA("sp", mats[:, k * 4096:(k + 1) * 4096], ssm_mats[k], mats, rd=[r_mats], wr=[mats])
                STAGE = "A"
                with ExitStack() as st:
                    xst = [sb(st, f"xst{i}", [128, D], F32, dma=True) for i in range(3)]
                    htm = [sb(st, f"htm{i}", [128, D], BF16) for i in range(2)]
                    junk_ = sb(st, "junk", [128, D], BF16)
                    wks = [(junk_, sb(st, f"ss{i}", [128, 1], F32), sb(st, f"ss2{i}", [128, 1], F32),
                            sb(st, f"rstd{i}", [128, 1], F32)) for i in range(2)]

                    def a_load(tt):
                        xs = xst[tt % 3]
                        DMA("sp", xs[:], x[row0 + tt * 128: row0 + (tt + 1) * 128, :], xs, wr=[xs])

                    def a_front(tt):
                        rmsnorm_tile(xst[tt % 3], 0, htm[tt % 2], wks[tt % 2])

                    def a_back(tt):
                        ht = htm[tt % 2]
                        bk = banks[6 + (tt % 2)]
                        bv = bfv(bk)
                        for kc in range(8):
                            TR(bv[:, kc * 128:(kc + 1) * 128], ht[:, kc * 128:(kc + 1) * 128], identb[:],
                               [ht, identb], [bk], signal=(kc == 7))
                        CP("act", H3[:, :, tt * 128:(tt + 1) * 128], bv.rearrange("p (k t) -> p k t", k=8),
                           [bk], [h_fm])
                    a_load(0)
                    a_load(1)
                    a_front(0)
                    for tt in range(16):
                        if tt + 2 < 16:
                            a_load(tt + 2)
                        if tt + 1 < 16:
                            a_front(tt + 1)
                        a_back(tt)
                    P.barrier(skip=[w_.r for w_ in wg_sb] + [wus.r, mats.r])
                    P.emit()
                    maybe_stop(f"{STAGE}{q}")

                STAGE = "B"
                with ExitStack() as st:
                    qk32s = [sb(st, f"qk32r_{i}", [128, 128], F32) for i in range(3)]
                    qktms = [sb(st, f"qktm_{i}", [128, 512], BF16) for i in range(3)]
                    qkT = [sb(st, f"qkT{i}", [128, 512], BF16) for i in range(3)]
                    vaug = [sb(st, f"vaug{i}", [128, 512], BF16) for i in range(6)]
                    pps = [sb(st, f"pp{i}", [128, 1024], BF16) for i in range(2)]
                    mk = sb(st, "mk", [128, 256], BF16)
                    acc = sb(st, "acc", [128, 4 * S], F32)
                    onesE = sb(st, "onesE", [128, 128], BF16)
                    onesO = sb(st, "onesO", [128, 128], BF16)
                    rps = [[sb(st, f"rp{i}_{k}", [128, 128], F32) for i in range(2)] for k in range(3)]
                    AC4 = acc[:].rearrange("p (k t) -> p k t", k=4)
                    for v_ in vaug:
                        MSET("pool", v_[:], 0.0, [v_])
                    MSET("pool", onesE[:], 0.0, [onesE])
                    MSET("pool", onesO[:], 0.0, [onesO])
                    MSET("pool", onesE[:, 0:64], 1.0, [onesE])
                    MSET("pool", onesO[:, 64:128], 1.0, [onesO])
                    CP("pool", mk[:, 0:128], maskc[:], [maskc], [mk])
                    CP("pool", mk[:, 128:256], maskp[:], [maskp], [mk])
                    blocks = []
                    for gi, (window, dil) in enumerate(GROUPS):
                        nb = (S // dil) // 128
                        for r_ in range(dil):
                            for n_ in range(nb):
                                blocks.append((gi, dil, r_, n_, r_ * nb + n_, len(blocks)))
                    pbanks = [(banks[0], banks[1]), (banks[6], banks[7])]
                    bSc = [banks[3], banks[4]]
                    PP5s = [p_[:].rearrange("p (c j two q) -> p c j two q", c=2, j=2, two=2) for p_ in pps]
                    PP4s = [p_[:].rearrange("p (c h q) -> p c h q", c=2, h=4) for p_ in pps]

                    def tokv(B, ap3):
                        gi, dil, r_, n_, blk, ix = B
                        return ap3.rearrange("p k (n i d) -> p k d n i", d=dil, i=128)[:, :, r_, n_, :]

                    def proj(B):
                        gi, dil, r_, n_, blk, ix = B
                        wg, W3 = wg_sb[gi], W3s[gi]
                        hblk = tokv(B, H3)
                        bA, bB = pbanks[ix % 2]
                        for kc in range(8):
                            MM(bA[:, 0:384], hblk[:, kc, :], W3[:, kc, 0:384], kc == 0, kc == 7,
                               [h_fm, wg], [bA], signal=False)
                        for kc in range(8):
                            MM(bB[:, 0:384], hblk[:, kc, :], W3[:, kc, 384:768], kc == 0, kc == 7,
                               [h_fm, wg], [bB], signal=(kc == 7))

                    def evac_rope(B):
                        gi, dil, r_, n_, blk, ix = B
                        bA, bB = pbanks[ix % 2]
                        q32, qktm, rp = qk32s[ix % 3], qktms[ix % 3], rps[ix % 3]
                        CP("act", qktm[:, 0:384], bA[:, 0:384], [bA], [qktm])
                        CP("act", qktm[:, 384:512], bB[:, 0:128], [bB], [qktm])
                        va = vaug[ix % 6]
                        VA4 = va[:].rearrange("p (j two c) -> p j two c", j=2, two=2)
                        vsrc = bB[:, 128:384].rearrange("p (j two e) -> p j two e", j=2, two=2)
                        CP("act", VA4[:, :, 0, 0:64], vsrc[:, :, 0, :], [bB], [va])
                        CP("act", VA4[:, :, 1, 64:128], vsrc[:, :, 1, :], [bB], [va])
                        q8 = q32[:].rearrange("p (h e) -> p h e", h=8)
                        CP("act", q8[:, 0:6, :], bA[:, 0:384].rearrange("p (h e) -> p h e", h=6)[:, :, 0:16], [bA], [q32])
                        CP("act", q8[:, 6:8, :], bB[:, 0:128].rearrange("p (h e) -> p h e", h=2)[:, :, 0:16], [bB], [q32])
                        o8 = qktm[:].rearrange("p (h e) -> p h e", h=8)
                        tix = (gi * 16 + blk) * 16
                        cc = ropec[:, tix:tix + 16].unsqueeze(1).to_broadcast([128, 8, 16])
                        ss_lo = ropes[:, tix:tix + 8].unsqueeze(1).to_broadcast([128, 8, 8])
                        ss_hi = ropes[:, tix + 8:tix + 16].unsqueeze(1).to_broadcast([128, 8, 8])
                        ma = rp[0][:].rearrange("p (h e) -> p h e", h=8)
                        mb = rp[1][:].rearrange("p (h e) -> p h e", h=8)
                        TT("dve", ma, q8, cc, ALU.mult, [q32, ropec], [rp[0]])
                        TT("dve", mb[:, :, 0:8], q8[:, :, 8:16], ss_lo, ALU.mult, [q32, ropes], [rp[1]])
                        TT("dve", mb[:, :, 8:16], q8[:, :, 0:8], ss_hi, ALU.mult, [q32, ropes], [rp[1]])
                        TT("dve", o8[:, :, 0:16], ma, mb, ALU.add, [rp[0], rp[1]], [qktm])

                    def transp(B):
                        gi, dil, r_, n_, blk, ix = B
                        qktm = qktms[ix % 3]
                        bT = banks[2]
                        bTv = bfv(bT)
                        for j in range(4):
                            TR(bTv[:, j * 128:(j + 1) * 128], qktm[:, j * 128:(j + 1) * 128], identb[:],
                               [qktm, identb], [bT], signal=(j == 3))
                        CP("dve", qkT[ix % 3][:], bTv[:, 0:512], [bT], [qkT[ix % 3]])

                    def scores(B):
                        gi, dil, r_, n_, blk, ix = B
                        kc_, kp_ = qkT[ix % 3], qkT[(ix - 1) % 3]
                        srcs = [(kc_, 0)] + ([(kp_, 256)] if n_ > 0 else [])
                        nmm = 4 * len(srcs)
                        cnt_ = 0
                        for (kt, off) in srcs:
                            for h in range(4):
                                pb = (h % 2) * 64
                                jj = h // 2
                                qs = kc_[pb:pb + 64, jj * 128:(jj + 1) * 128]
                                kk_ = kt[pb:pb + 64, (2 + jj) * 128:(3 + jj) * 128]
                                cnt_ += 1
                                MM(bSc[h % 2][:, off + jj * 128:off + (jj + 1) * 128], kk_, qs, True, True,
                                   [kc_, kt], [bSc[h % 2]], signal=(cnt_ >= nmm - 1))

                    def softmax(B):
                        gi, dil, r_, n_, blk, ix = B
                        pp, PP5, PP4 = pps[ix % 2], PP5s[ix % 2], PP4s[ix % 2]
                        if n_ > 0:
                            for par in range(2):
                                ACT(PP5[:, :, :, par, :], bSc[par][:, :].rearrange("p (c j q) -> p c j q", c=2, j=2),
                                    AF.Exp, [bSc[par]], [pp], scale=0.125)
                            TT("pool", PP4, PP4,
                               mk[:].rearrange("p (c q) -> p c q", c=2).unsqueeze(2).to_broadcast([128, 2, 4, 128]),
                               ALU.mult, [pp, mk], [pp])
                        else:
                            for par in range(2):
                                ACT(PP5[:, 0, :, par, :], bSc[par][:, 0:256].rearrange("p (j q) -> p j q", j=2),
                                    AF.Exp, [bSc[par]], [pp], scale=0.125)
                            TT("pool", PP4[:, 0], PP4[:, 0],
                               mk[:, 0:128].unsqueeze(1).to_broadcast([128, 4, 128]), ALU.mult, [pp, mk], [pp])

                    def pv_acc(B):
                        gi, dil, r_, n_, blk, ix = B
                        bV = banks[5]
                        pp, PP4 = pps[ix % 2], PP4s[ix % 2]
                        kbs = ([(vaug[(ix - 1) % 6], 1)] if n_ > 0 else []) + [(vaug[ix % 6], 0)]
                        for j in range(2):
                            seq = [(vt, cp, h) for (vt, cp) in kbs for h in (2 * j, 2 * j + 1)]
                            for i_, (vt, cp, h) in enumerate(seq):
                                MM(bV[:, j * 128:(j + 1) * 128], vt[:, h * 128:(h + 1) * 128],
                                   PP4[:, cp, h, :], i_ == 0, i_ == len(seq) - 1, [vt, pp], [bV], signal=False)
                            for i_, (vt, cp, h) in enumerate(seq):
                                oz = onesE if h % 2 == 0 else onesO
                                MM(bV[:, 256 + j * 128:256 + (j + 1) * 128], oz[:],
                                   PP4[:, cp, h, :], i_ == 0, i_ == len(seq) - 1, [oz, pp], [bV],
                                   signal=(j == 1 and i_ == len(seq) - 1))
                        src = bV[:, :].rearrange("p (k q) -> p k q", k=4)
                        dst = tokv(B, AC4)
                        if gi == 0:
                            CP("dve", dst, src, [bV], [acc])
                        else:
                            TT("dve", dst, src, dst, ALU.add, [bV, acc], [acc])

                    nblk = len(blocks)
                    for k_ in range(3):
                        proj(blocks[k_])
                        evac_rope(blocks[k_])
                        if k_ == 0:
                            transp(blocks[0])
                    for i_b, B in enumerate(blocks):
                        scores(B)
                        softmax(B)
                        if i_b + 1 < nblk:
                            transp(blocks[i_b + 1])
                        if i_b >= 1:
                            pv_acc(blocks[i_b - 1])
                        if i_b + 3 < nblk:
                            proj(blocks[i_b + 3])
                            evac_rope(blocks[i_b + 3])
                    pv_acc(blocks[nblk - 1])
                    for hh in range(2):
                        dsl = slice(2 * S + hh * S, 2 * S + (hh + 1) * S)
                        ACT(acc[:, dsl], acc[:, dsl], AF.Ln, [acc], [acc])
                        ACT(acc[:, dsl], acc[:, dsl], AF.Exp, [acc], [acc], scale=-1.0)
                    TT("dve", attn_fm[:], acc[:, 0:2 * S], acc[:, 2 * S:4 * S], ALU.mult, [acc], [attn_fm])
                    if debug and q == 0:
                        DMA("sp", dbg_attn, attn_fm[:], dbg_slot, rd=[attn_fm])
                    P.barrier(skip=[wus.r, mats.r])
                    P.emit()
                    maybe_stop(f"{STAGE}{q}")
                sab.close()

                STAGE = "C"
                with ExitStack() as st:
                    Y = sb(st, "Y", [128, 2 * 8 * 512], BF16)
                    Y4 = Y[:].rearrange("p (mt s f) -> p mt s f", mt=2, s=8)
                    YI = Y[:].rearrange("p (mt g s c) -> p mt g s c", mt=2, g=32, s=8)
                    U = sb(st, "U", [128, 32 * 256], BF16)
                    U3 = U[:].rearrange("p (g m) -> p g m", g=32)
                    Xp = sb(st, "Xp", [128, 32 * 256], BF16)
                    X3 = Xp[:].rearrange("p (g m) -> p g m", g=32)
                    tabs2 = [[sb(st, f"tab{k}_{j}", [128, 1024], F32, dma=True) for k in range(3)] for j in range(2)]
                    m1 = sb(st, "m1", [128, 1024], F32)
                    m2 = sb(st, "m2", [128, 1024], F32)
                    zz = sb(st, "zz", [128, 1024], F32)
                    zs = sb(st, "zs", [128, 1024], F32)
                    HS = H3.rearrange("p k (mt i s) -> p k mt s i", mt=2, s=8)
                    for mt in range(2):
                        for s_ in range(8):
                            bk = banks[(mt * 8 + s_) % 2]
                            for kc in range(8):
                                MM(bk[:, :], HS[:, kc, mt, s_, :], WU3[:, kc, :], kc == 0, kc == 7,
                                   [h_fm, wus], [bk], signal=(kc == 7))
                            CP("act", YI[:, mt, :, s_, :], bk[:, :].rearrange("p (g c) -> p g c", g=32), [bk], [Y])
                    Y5 = Y[:].rearrange("p (mt s g c) -> p mt s g c", mt=2, s=8, g=32)
                    YF = Y[:].rearrange("p (mt g f) -> p mt g f", mt=2, g=32)
                    for g4 in range(8):
                        bk = banks[2 + (g4 % 2)]
                        bv = bfv(bk)
                        for gl in range(4):
                            g = g4 * 4 + gl
                            for mt in range(2):
                                o_ = (gl * 2 + mt) * 128
                                TR(bv[:, o_:o_ + 128], YF[:, mt, g, :], identb[:], [Y, identb], [bk],
                                   signal=(gl == 3 and mt == 1))
                        CP("dve", U[:, g4 * 1024:(g4 + 1) * 1024], bv[:, :], [bk], [U])
                    MSET("pool", X3[:, :, 0:1], 0.0, [Xp])
                    Yf = Y.t[:].bitcast(F32)
                    wsets = [(m1, m2, zz, zs),
                             tuple(T(Yf[:, k_ * 1024:(k_ + 1) * 1024], P.res(f"yalias{k_}")) for k_ in range(4))]
                    bG = [banks[4], banks[5]]
                    bS = [banks[6], banks[7]]

                    def c_tabs(g4n):
                        csn = slice(g4n * 1024, (g4n + 1) * 1024)
                        for k in range(3):
                            tn = tabs2[g4n % 2][k]
                            DMA("sp", tn[:], ssm_tab[k, :, csn], tn, rd=[r_tab], wr=[tn])

                    def c_first(g4):
                        tabs = tabs2[g4 % 2]
                        w1, w2, wz, wzs = wsets[g4 % 2]
                        for gl in range(4):
                            g = g4 * 4 + gl
                            MM(bG[gl // 2][:, (gl % 2) * 256:(gl % 2 + 1) * 256], MT4[:, 0, g, :], U3[:, g, :],
                               True, True, [mats, U], [bG[gl // 2]], signal=False)
                            MM(bS[gl // 2][:, (gl % 2) * 256:(gl % 2 + 1) * 256], MT4[:, 1, g, :], U3[:, g, :],
                               True, True, [mats, U], [bS[gl // 2]], signal=True)
                        for hb in range(2):
                            fs = slice(hb * 512, (hb + 1) * 512)
                            extra = [Y] if (g4 == 1 and hb == 0) else []
                            TT("dve", w1[:, fs], bG[hb][:, :], tabs[0][:, fs], ALU.mult, [bG[hb], tabs[0]], [w1] + extra)
                            TT("dve", w2[:, fs], bS[hb][:, :], tabs[1][:, fs], ALU.mult, [bS[hb], tabs[1]], [w2])
                        TT("dve", w1[:], w1[:], w2[:], ALU.add, [w1, w2], [w1])
                        P.op("dve", lambda e, tb_=tabs[2], wz_=wz, w1_=w1: e.tensor_tensor_scan(
                            out=wz_[:], data0=tb_[:], data1=w1_[:], initial=0.0, op0=ALU.mult, op1=ALU.add),
                            R([tabs[2], w1]), R([wz]))

                    def c_second(g4):
                        tabs = tabs2[g4 % 2]
                        w1, w2, wz, wzs = wsets[g4 % 2]
                        CP("act", wzs[0:64, :], wz[64:128, :], [wz], [wzs])
                        CP("act", wzs[64:128, :], wz[0:64, :], [wz], [wzs])
                        TT("dve", w1[:], wz[:], tabs[0][:], ALU.mult, [wz, tabs[0]], [w1])
                        TT("pool", w2[:], wzs[:], tabs[1][:], ALU.mult, [wzs, tabs[1]], [w2])
                        extra = [Y] if g4 == 7 else []
                        TT("dve", X3[:, g4 * 4:(g4 + 1) * 4, 1:256],
                           w1[:].rearrange("p (g m) -> p g m", g=4)[:, :, 0:255],
                           w2[:].rearrange("p (g m) -> p g m", g=4)[:, :, 0:255], ALU.subtract, [w1, w2], [Xp] + extra)

                    c_tabs(0)
                    c_tabs(1)
                    c_first(0)
                    for g4 in range(8):
                        if g4 + 1 < 8:
                            c_first(g4 + 1)
                        c_second(g4)
                        if g4 + 2 < 8:
                            c_tabs(g4 + 2)
                    for mt in range(2):
                        for g4 in range(8):
                            bk = banks[(g4 % 2)]
                            for gl in range(4):
                                g = g4 * 4 + gl
                                MM(bk[:, gl * 128:(gl + 1) * 128], U3[:, g, mt * 128:(mt + 1) * 128], MT4[:, 2, g, :],
                                   True, False, [U, mats], [bk], signal=False)
                                MM(bk[:, gl * 128:(gl + 1) * 128], X3[:, g, mt * 128:(mt + 1) * 128], MT4[:, 3, g, :],
                                   False, True, [Xp, mats], [bk], signal=(gl == 3))
                            ACT(Y5[:, mt, :, g4 * 4:(g4 + 1) * 4, :].rearrange("p s g c -> p g s c"),
                                bk[:, :].rearrange("p (g s c) -> p g s c", g=4, s=8), AF.Gelu, [bk], [Y])
                    for mt in range(2):
                        for ch in range(4):
                            bk = banks[2 + ((mt * 4 + ch) % 2)]
                            bv = bfv(bk)
                            for s_ in range(8):
                                TR(bv[:, s_ * 128:(s_ + 1) * 128], Y4[:, mt, s_, ch * 128:(ch + 1) * 128], identb[:],
                                   [Y, identb], [bk], signal=(s_ == 7))
                            CP("dve", YA3[:, ch, mt * 1024:(mt + 1) * 1024].rearrange("p (m s) -> p s m", s=8),
                               bv.rearrange("p (s m) -> p s m", s=8), [bk], [yact_fm])
                    if debug and q == 0:
                        DMA("sp", dbg_y, yact_fm[:], dbg_slot, rd=[yact_fm])
                    P.barrier()
                    P.emit()
                    maybe_stop(f"{STAGE}{q}")
                sseq.close()

                STAGE = "D"
                with ExitStack() as st:
                    wgt = sb(st, "wgt", [128, 8 * 2048], BF16, dma=True)
                    WG3 = wgt[:].rearrange("p (k c) -> p k c", k=8)
                    DMA("pool", WG3[:, 0:4, :], wgate[0:512].rearrange("(k p) c -> p k c", p=128), wgt, wr=[wgt])
                    DMA("pool", WG3[:, 4:8, :], wgate[512:1024].rearrange("(k p) c -> p k c", p=128), wgt, wr=[wgt])
                    wgl = sb(st, "wgl", [128, 4 * 2048], BF16, dma=True)
                    WL3 = wgl[:].rearrange("p (k c) -> p k c", k=4)
                    DMA("pool", WL3, wglu.rearrange("(k p) c -> p k c", p=128), wgl, wr=[wgl])
                    waos = sb(st, "waos", [128, 2 * D], BF16, dma=True)
                    WA3 = waos[:].rearrange("p (k c) -> p k c", k=2)
                    DMA("pool", WA3, wao.rearrange("(k p) c -> p k c", p=128), waos, wr=[waos])
                    wos = sb(st, "wos", [128, 8 * D], BF16, dma=True)
                    WO3 = wos[:].rearrange("p (k c) -> p k c", k=8)
                    DMA("pool", WO3, wout.rearrange("(k p) c -> p k c", p=128), wos, wr=[wos])
                    mrg = sb(st, "mrg", [128, 8 * 512], BF16)
                    MG3 = mrg[:].rearrange("p (k t) -> p k t", k=8)
                    sA = sb(st, "sA", [128, 512], F32)
                    sB = sb(st, "sB", [128, 512], F32)
                    sE = sb(st, "sE", [128, 512], F32)
                    xst = [sb(st, f"xsd{i}", [128, D], F32, dma=True) for i in range(2)]
                    x1s = [sb(st, f"x1s{i}", [128, D], F32, dma=True) for i in range(2)]
                    for tb in range(4):
                        tsl = slice(tb * 512, (tb + 1) * 512)
                        for dc in range(8):
                            par = dc % 2
                            bG0, bG1, bAD = banks[0], banks[1], banks[2]
                            bZA, bZB = banks[3], banks[4]
                            c0 = slice(dc * 128, (dc + 1) * 128)
                            c1 = slice(1024 + dc * 128, 1024 + (dc + 1) * 128)
                            for kc in range(8):
                                MM(bG0[:, :], WG3[:, kc, c0], H3[:, kc, tsl], kc == 0, kc == 7, [wgt, h_fm], [bG0], signal=(kc == 7))
                            for kc in range(8):
                                MM(bG1[:, :], WG3[:, kc, c1], H3[:, kc, tsl], kc == 0, kc == 7, [wgt, h_fm], [bG1], signal=(kc == 7))
                            for kc in range(2):
                                MM(bAD[:, :], WA3[:, kc, c0], AT3[:, kc, tsl], kc == 0, kc == 1, [waos, attn_fm], [bAD], signal=(kc == 1))
                            for kc in range(4):
                                MM(bZA[:, :], WL3[:, kc, c0], YA3[:, kc, tsl], kc == 0, kc == 3, [wgl, yact_fm], [bZA], signal=(kc == 3))
                            for kc in range(4):
                                MM(bZB[:, :], WL3[:, kc, c1], YA3[:, kc, tsl], kc == 0, kc == 3, [wgl, yact_fm], [bZB], signal=(kc == 3))
                            ACT(sA[:], bG0[:, :], AF.Sigmoid, [bG0], [sA])
                            ACT(sB[:], bG1[:, :], AF.Sigmoid, [bG1], [sB])
                            ACT(sE[:], bZB[:, :], AF.Sigmoid, [bZB], [sE])
                            TT("dve", sA[:], sA[:], bAD[:, :], ALU.mult, [sA, bAD], [sA])
                            TT("dve", sE[:], sE[:], bZA[:, :], ALU.mult, [sE, bZA], [sE])
                            TT("dve", sE[:], sE[:], sB[:], ALU.mult, [sE, sB], [sE])
                            TT("dve", MG3[:, dc, :], sA[:], sE[:], ALU.add, [sA, sE], [mrg])
                        for i in range(4):
                            tt = tb * 4 + i
                            xs = xst[tt % 2]
                            x1 = x1s[tt % 2]
                            rows = slice(row0 + tt * 128, row0 + (tt + 1) * 128)
                            DMA("sp", xs[:], x[rows, :], xs, wr=[xs])
                            bO = [banks[5], banks[6]]
                            for half in range(2):
                                for kc in range(8):
                                    MM(bO[half][:, :], MG3[:, kc, i * 128:(i + 1) * 128], WO3[:, kc, half * 512:(half + 1) * 512],
                                       kc == 0, kc == 7, [mrg, wos], [bO[half]], signal=(kc == 7))
                            for half in range(2):
                                hs = slice(half * 512, (half + 1) * 512)
                                TT("dve", x1[:, hs], bO[half][:, :], xs[:, hs], ALU.add, [bO[half], xs], [x1])
                            DMA("sp", x1buf[rows, :], x1[:], x1, rd=[x1], wr=[r_x1])
                    P.barrier()
                    P.emit()
                    maybe_stop(f"{STAGE}{q}")

        with ExitStack() as st:
            wgs = sb(st, "wgs", [128, 8 * DFF], BF16, dma=True)
            wus2 = sb(st, "wus2", [128, 8 * DFF], BF16, dma=True)
            wds = sb(st, "wds", [128, NFC * D], BF16, dma=True)
            WGF = wgs[:].rearrange("p (k c) -> p k c", k=8)
            WUF = wus2[:].rearrange("p (k c) -> p k c", k=8)
            WDF = wds[:].rearrange("p (k c) -> p k c", k=NFC)
            for kc in range(8):
                DMA("pool", WGF[:, kc, :], wfg[kc * 128:(kc + 1) * 128, :], wgs, wr=[wgs])
                DMA("pool", WUF[:, kc, :], wfu[kc * 128:(kc + 1) * 128, :], wus2, wr=[wus2])
            for c2 in range(0, NFC, 2):
                DMA("pool", WDF[:, c2:c2 + 2, :], wfd[c2 * 128:(c2 + 2) * 128, :].rearrange("(k p) c -> p k c", p=128), wds, wr=[wds])
            cut("f0")
            x1t = [sb(st, f"x1t{i}", [128, D], F32, dma=True) for i in range(4)]
            h2tm = [sb(st, f"h2tm{i}", [128, D], BF16) for i in range(2)]
            h2fm = sb(st, "h2fm", [128, 8 * 512], BF16)
            H2 = h2fm[:].rearrange("p (k t) -> p k t", k=8)
            actb = sb(st, "actb", [128, NFC * 512], BF16)
            AC3 = actb[:].rearrange("p (k t) -> p k t", k=NFC)
            sg = [sb(st, f"sg{i}", [128, 512], F32) for i in range(1)]
            junk2_ = sb(st, "junk2", [128, D], BF16)
            wks = [(junk2_, sb(st, f"ss_{i}", [128, 1], F32), sb(st, f"ss2_{i}", [128, 1], F32),
                    sb(st, f"rstd_{i}", [128, 1], F32)) for i in range(4)]
            NB = NSEQ * S // 512
            xf = x1t[0:2]
            xr = x1t[2:4]
            wks_f, wks_b = wks[0:2], wks[2:4]

            def rows_of(blk, i):
                tt = blk * 4 + i
                return slice(tt * 128, (tt + 1) * 128)

            def ff_load(blk, i):
                DMA("sp", xf[i % 2][:], x1buf[rows_of(blk, i), :], xf[i % 2], rd=[r_x1], wr=[xf[i % 2]])

            def ff_norm(blk, i):
                rmsnorm_tile(xf[i % 2], 1, h2tm[i % 2], wks_f[i % 2])

            def ff_tr(blk, i):
                ht = h2tm[i % 2]
                bk = banks[6 + (i % 2)]
                bv = bfv(bk)
                for kc in range(8):
                    TR(bv[:, kc * 128:(kc + 1) * 128], ht[:, kc * 128:(kc + 1) * 128], identb[:],
                       [ht, identb], [bk], signal=(kc == 7))
                CP("act", H2[:, :, i * 128:(i + 1) * 128], bv.rearrange("p (k t) -> p k t", k=8), [bk], [h2fm])

            def rr_load(blk, i):
                DMA("sp", xr[i % 2][:], x1buf[rows_of(blk, i), :], xr[i % 2], rd=[r_x1], wr=[xr[i % 2]])

            def down(blk, i):
                xt = xr[i % 2]
                bO = [banks[4], banks[5]]
                for half in range(2):
                    for c in range(NFC):
                        MM(bO[half][:, :], AC3[:, c, i * 128:(i + 1) * 128], WDF[:, c, half * 512:(half + 1) * 512],
                           c == 0, c == NFC - 1, [actb, wds], [bO[half]], signal=(c == NFC - 1))
                for half in range(2):
                    hs = slice(half * 512, (half + 1) * 512)
                    TT("dve", xt[:, hs], bO[half][:, :], xt[:, hs], ALU.add, [bO[half], xt], [xt])
                rmsnorm_tile(xt, 2, xt, wks_b[i % 2])
                out_toks.append(DMA("sp", out[rows_of(blk, i), :], xt[:], xt, rd=[xt]))

            ff_load(0, 0)
            ff_load(0, 1)
            ff_norm(0, 0)
            ff_load(0, 2)
            ff_norm(0, 1)
            ff_load(0, 3)
            for i in range(4):
                ff_tr(0, i)
                if i + 2 < 4:
                    ff_norm(0, i + 2)
            for blk in range(NB):
                nb_ = blk + 1 if blk + 1 < NB else None
                rr_load(blk, 0)
                rr_load(blk, 1)
                if nb_ is not None:
                    ff_load(nb_, 0)
                    ff_load(nb_, 1)
                for c in range(NFC):
                    bg = banks[c % 2]
                    bu = banks[2 + (c % 2)]
                    cs = slice(c * 128, (c + 1) * 128)
                    for kc in range(8):
                        MM(bg[:, :], WGF[:, kc, cs], H2[:, kc, :], kc == 0, kc == 7, [wgs, h2fm], [bg], signal=(kc == 7))
                    for kc in range(8):
                        MM(bu[:, :], WUF[:, kc, cs], H2[:, kc, :], kc == 0, kc == 7, [wus2, h2fm], [bu], signal=(kc == 7))
                    s_ = sg[0]
                    ACT(s_[:], bg[:, :], AF.Silu, [bg], [s_])
                    TT("dve", AC3[:, c, :], s_[:], bu[:, :], ALU.mult, [s_, bu], [actb])
                    if nb_ is not None and c == 17:
                        ff_norm(nb_, 0)
                        ff_load(nb_, 2)
                    if nb_ is not None and c == 19:
                        ff_norm(nb_, 1)
                        ff_load(nb_, 3)
                for i in range(4):
                    if nb_ is not None:
                        ff_tr(nb_, i)
                        if i + 2 < 4:
                            ff_norm(nb_, i + 2)
                    down(blk, i)
                    if i + 2 < 4:
                        rr_load(blk, i + 2)
            for t in out_toks[-2:]:
                P._wait("sp", t)
            P.barrier()
            P.emit()
    except _Stop:
        pass
    return nc


def _host_consts():
    bf = ml_dtypes.bfloat16
    c = {}
    c["c_ident"] = np.eye(128, dtype=np.float32)
    kk = np.arange(128)[:, None]
    qq = np.arange(128)[None, :]
    c["c_maskc"] = (kk <= qq).astype(np.float32).astype(bf)
    c["c_maskp"] = (kk >= qq).astype(np.float32).astype(bf)
    half = 8
    inv = np.power(np.float32(500000.0), -np.arange(half, dtype=np.float32) * np.float32(2.0) / np.float32(16.0)).astype(np.float32)
    rc = np.zeros((128, 3, 16, 8), np.float32)
    rs = np.zeros((128, 3, 16, 8), np.float32)
    i = np.arange(128)
    for gi, (_, dil) in enumerate(GROUPS):
        nb = (S // dil) // 128
        for r_ in range(dil):
            for n_ in range(nb):
                blk = r_ * nb + n_
                pos = (r_ + dil * (128 * n_ + i)).astype(np.float32)
                ang = pos[:, None] * inv[None, :]
                rc[:, gi, blk, :] = np.cos(ang)
                rs[:, gi, blk, :] = np.sin(ang)
    c["c_ropec"] = np.concatenate([rc, rc], axis=3).reshape(128, 768)
    c["c_ropes"] = np.concatenate([-rs, rs], axis=3).reshape(128, 768)
    jv = np.repeat(np.arange(-7, 16, dtype=np.float32)[None, :, None], 32, axis=2)
    c["c_jv"] = np.broadcast_to(jv, (128, NJ, 32)).reshape(128, NJ * 32).copy()
    c["c_mrow"] = np.broadcast_to(np.arange(256, dtype=np.float32)[None, :], (128, 256)).copy()
    st = np.ones(256, np.float32)
    st[0] = 0.0
    c["c_step"] = np.broadcast_to(st[None, :], (128, 256)).copy()
    s_i = (np.arange(128) // 16)[:, None]
    t_i = (np.arange(128) // 16)[None, :]
    c["c_tmask"] = (t_i >= s_i).astype(np.float32)
    return c


def _prep_inputs(inputs):
    f = lambda a: np.ascontiguousarray(np.asarray(a, dtype=np.float32))
    w_in = f(inputs["w_in"])[0]
    shared = {}
    wq = np.zeros((3, D, 768), np.float32)
    for gi in range(3):
        cols = []
        for t3 in range(3):
            for h in range(4):
                head = gi * 4 + h
                base = (t3 * 12 + head) * 64
                cols.append(np.arange(base, base + 64))
        wq[gi] = w_in[:, np.concatenate(cols)]
    shared["wqkv"] = wq
    shared["wu"] = f(w_in[:, 2304:2816])
    shared["wgate"] = f(w_in[:, 2816:4864])
    shared["wglu"] = f(inputs["w_glu"])[0]
    shared["wao"] = f(inputs["w_attn_out"])[0]
    shared["wout"] = f(inputs["w_out"])[0]
    shared["wfg"] = f(inputs["w_ffn_gate"])[0]
    shared["wfu"] = f(inputs["w_ffn_up"])[0]
    shared["wfd"] = f(inputs["w_ffn_down"])[0]
    shared["gains"] = np.stack([f(inputs["norm_mix_g"])[0], f(inputs["norm_ffn_g"])[0], f(inputs["norm_final_g"])], 0)
    shared["a_re"] = f(inputs["ssm_a_re"])[0]
    shared["a_im"] = f(inputs["ssm_a_im"])[0]
    shared["log_dt"] = f(inputs["ssm_log_dt"]).reshape(1, 32)
    shared["b_re"] = f(inputs["ssm_b_re"])[0]
    shared["b_im"] = f(inputs["ssm_b_im"])[0]
    shared["c_re"] = f(inputs["ssm_c_re"])[0].reshape(512, 64)
    shared["c_im"] = f(inputs["ssm_c_im"])[0].reshape(512, 64)
    shared["dsk"] = f(inputs["ssm_d"])[0]
    shared.update(_host_consts())
    xs = f(inputs["x"]).reshape(NCORES, NSEQ * S, D)
    return [dict(shared, x=xs[i]) for i in range(NCORES)]


def kernel(**inputs):
    in_maps = _prep_inputs(inputs)
    nc = build_program()
    res = run_bass_kernel_spmd(nc, in_maps, core_ids=list(range(NCORES)))
    outs = [np.asarray(r["out"], dtype=np.float32) for r in res.results]
    return np.concatenate(outs, axis=0).reshape(16, S, D)
```
